# Optimizing a Trainium2 kernel written in Bass

```python
import math
import jax, jax.numpy as jnp
from jax import lax
import numpy as np

D_MODEL = 1024
BATCH = 16
SEQ = 2048
DEPTH = 4

HEAD_DIM = 64
ROPE_DIM = HEAD_DIM // 4
ROPE_THETA = 500000.0
Q_BLOCK = 128
LN_EPS = 1e-5
DIFF_HEADS = 4
NSA_HEADS = 8
NSA_KV_HEADS = 2
NSA_CMP_LEN = 32
NSA_CMP_STRIDE = 16
NSA_CMP_HIDDEN = 256
NSA_SEL_LEN = 64
NSA_TOPK = 8
NSA_WINDOW = 512
NSA_Q_BLOCK = 64
NSA_N_BRANCH = 3
NSA_BIG = 1e4
NEG = -1e30
FOX_HEADS = D_MODEL // HEAD_DIM
FOX_WIDTH = FOX_HEADS * HEAD_DIM
D_FF = 4 * D_MODEL
ALPHA = (2 * DEPTH) ** 0.25
BETA = (8 * DEPTH) ** -0.25
N_EVEN = (DEPTH + 1) // 2
N_ODD = DEPTH // 2

DIFF_QK = DIFF_HEADS * 2 * HEAD_DIM
DIFF_V = DIFF_HEADS * 2 * HEAD_DIM
NSA_Q = NSA_HEADS * HEAD_DIM
NSA_KV = NSA_KV_HEADS * HEAD_DIM
EVEN_SPLITS = (DIFF_QK, DIFF_QK, DIFF_V, NSA_Q, NSA_KV, NSA_KV, NSA_KV, NSA_KV, NSA_KV, NSA_KV, NSA_HEADS * NSA_N_BRANCH)
EVEN_COLS = sum(EVEN_SPLITS)
EVEN_MIX = DIFF_V + NSA_Q
ODD_SPLITS = (FOX_WIDTH, FOX_WIDTH, FOX_WIDTH, FOX_HEADS)
ODD_COLS = sum(ODD_SPLITS)

kernel_name = "hybrid_diff_nsa_fox_deepnorm"


def _split(h, sizes):
    offsets = [int(o) for o in np.cumsum(sizes)[:-1]]
    return jnp.split(h, offsets, axis=-1)


def layer_norm(x, g, b):
    xf = x.astype(jnp.float32)
    mu = jnp.mean(xf, axis=-1, keepdims=True)
    var = jnp.mean(jnp.square(xf - mu), axis=-1, keepdims=True)
    return ((xf - mu) * lax.rsqrt(var + LN_EPS) * g.astype(jnp.float32) + b.astype(jnp.float32)).astype(x.dtype)


def rms_norm(x, g):
    xf = x.astype(jnp.float32)
    return xf * lax.rsqrt(jnp.mean(jnp.square(xf), axis=-1, keepdims=True) + LN_EPS) * g.astype(jnp.float32)


def partial_rope(x, pos):
    half = ROPE_DIM // 2
    inv = ROPE_THETA ** (-jnp.arange(half, dtype=jnp.float32) / half)
    ang = pos.astype(jnp.float32)[:, None] * inv[None, :]
    shape = (pos.shape[0],) + (1,) * (x.ndim - 3) + (half,)
    cos = jnp.cos(ang).reshape(shape)
    sin = jnp.sin(ang).reshape(shape)
    xf = x.astype(jnp.float32)
    x1 = xf[..., :half]
    x2 = xf[..., half:ROPE_DIM]
    return jnp.concatenate([x1 * cos - x2 * sin, x2 * cos + x1 * sin, xf[..., ROPE_DIM:]], axis=-1)


def _sweep(fn, seq_len, block):
    out = lax.map(fn, jnp.arange(seq_len // block))
    out = jnp.moveaxis(out, 0, 1)
    return out.reshape((out.shape[0], seq_len) + out.shape[3:])


def diff_attention(q, k, v, lam_params, subln_g, layer_idx):
    B, S = q.shape[:2]
    pos = jnp.arange(S)
    scale = HEAD_DIM ** -0.5
    qr = partial_rope(q, pos) * scale
    kr = partial_rope(k, pos)
    vf = v.astype(jnp.float32)
    lam_init = 0.8 - 0.6 * math.exp(-0.3 * layer_idx)
    lp = lam_params.astype(jnp.float32)
    lam = jnp.exp(jnp.sum(lp[0] * lp[1])) - jnp.exp(jnp.sum(lp[2] * lp[3])) + lam_init

    def blk(i):
        q0 = i * Q_BLOCK
        qb = lax.dynamic_slice_in_dim(qr, q0, Q_BLOCK, axis=1)
        tq = q0 + jnp.arange(Q_BLOCK)
        mask = pos[None, :] <= tq[:, None]
        s = jnp.einsum('bqhcd,bshcd->bhcqs', qb, kr)
        p = jax.nn.softmax(jnp.where(mask, s, -jnp.inf), axis=-1)
        pd = p[:, :, 0] - lam * p[:, :, 1]
        return jnp.einsum('bhqs,bshe->bqhe', pd, vf)

    o = _sweep(blk, S, Q_BLOCK)
    o = rms_norm(o, subln_g) * (1.0 - lam_init)
    return o.reshape(B, S, -1).astype(v.dtype)


def nsa_attention(q, k_cmp, v_cmp, k_slc, v_slc, k_win, v_win, gate_logits, pe, w1, b1, w2):
    B, S, H, d = q.shape
    G = NSA_KV_HEADS
    HPG = H // G
    W = NSA_WINDOW
    pos = jnp.arange(S)
    scale = HEAD_DIM ** -0.5
    qr = (partial_rope(q, pos) * scale).reshape(B, S, G, HPG, d)

    n_cmp = (S - NSA_CMP_LEN) // NSA_CMP_STRIDE + 1
    cmp_start = jnp.arange(n_cmp) * NSA_CMP_STRIDE
    cmp_end = cmp_start + NSA_CMP_LEN - 1
    cmp_idx = cmp_start[:, None] + jnp.arange(NSA_CMP_LEN)[None, :]

    def compress(t, j):
        blocks = t[:, cmp_idx] + pe[j][None, None, :, None, :]
        blocks = jnp.moveaxis(blocks, 3, 2).reshape(B, n_cmp, G, NSA_CMP_LEN * d)
        h = jax.nn.gelu(blocks @ w1[j] + b1[j])
        return h @ w2[j]

    kc = partial_rope(compress(k_cmp, 0), cmp_end)
    vc = compress(v_cmp, 1).astype(jnp.float32)

    n_sel = S // NSA_SEL_LEN
    topk = min(NSA_TOPK, n_sel)
    ks = partial_rope(k_slc, pos)
    ks_blk = ks.reshape(B, n_sel, NSA_SEL_LEN, G, d).transpose(0, 3, 1, 2, 4)
    vs_blk = v_slc.astype(jnp.float32).reshape(B, n_sel, NSA_SEL_LEN, G, d).transpose(0, 3, 1, 2, 4)
    sel_start = jnp.arange(n_sel) * NSA_SEL_LEN
    overlap = ((cmp_start[:, None] <= sel_start[None, :] + NSA_SEL_LEN - 1)
               & (cmp_end[:, None] >= sel_start[None, :])).astype(jnp.float32)
    bi = jnp.arange(B)[:, None, None, None]
    gi = jnp.arange(G)[None, :, None, None]
    jb = jnp.arange(n_sel)

    kw_pad = jnp.pad(partial_rope(k_win, pos), ((0, 0), (W, 0), (0, 0), (0, 0)))
    vw_pad = jnp.pad(v_win.astype(jnp.float32), ((0, 0), (W, 0), (0, 0), (0, 0)))

    gate = jax.nn.sigmoid(gate_logits.astype(jnp.float32)).reshape(B, S, G, HPG, NSA_N_BRANCH)

    def blk(i):
        q0 = i * NSA_Q_BLOCK
        qb = lax.dynamic_slice_in_dim(qr, q0, NSA_Q_BLOCK, axis=1)
        tq = q0 + jnp.arange(NSA_Q_BLOCK)
        valid_c = cmp_end[None, :] <= tq[:, None]
        sc = jnp.einsum('bqghd,bngd->bghqn', qb, kc)
        pc = jax.nn.softmax(jnp.where(valid_c, sc, NEG), axis=-1) * valid_c
        o_c = jnp.einsum('bghqn,bngd->bqghd', pc, vc)
        imp = jnp.einsum('bghqn,nj->bgqj', pc, overlap)
        cur = tq // NSA_SEL_LEN
        forced = (jb[None, :] == 0) | (jb[None, :] == cur[:, None]) | (jb[None, :] == cur[:, None] - 1)
        future = jb[None, :] > cur[:, None]
        imp = jnp.where(forced, NSA_BIG, jnp.where(future, -NSA_BIG, imp))
        _, idx = lax.top_k(imp, topk)
        kg = ks_blk[bi, gi, idx]
        vg = vs_blk[bi, gi, idx]
        tok = idx[..., None] * NSA_SEL_LEN + jnp.arange(NSA_SEL_LEN)
        valid_s = (tok <= tq[None, None, :, None, None])[:, :, None]
        ss = jnp.where(valid_s, jnp.einsum('bqghd,bgqkld->bghqkl', qb, kg), -jnp.inf)
        ps = jax.nn.softmax(ss.reshape(ss.shape[:4] + (-1,)), axis=-1).reshape(ss.shape)
        o_s = jnp.einsum('bghqkl,bgqkld->bqghd', ps, vg)
        kwb = lax.dynamic_slice_in_dim(kw_pad, q0, W + NSA_Q_BLOCK, axis=1)
        vwb = lax.dynamic_slice_in_dim(vw_pad, q0, W + NSA_Q_BLOCK, axis=1)
        sk = q0 - W + jnp.arange(W + NSA_Q_BLOCK)
        valid_w = (sk[None, :] <= tq[:, None]) & (sk[None, :] > tq[:, None] - W) & (sk[None, :] >= 0)
        sw = jnp.einsum('bqghd,bsgd->bghqs', qb, kwb)
        pw = jax.nn.softmax(jnp.where(valid_w, sw, -jnp.inf), axis=-1)
        o_w = jnp.einsum('bghqs,bsgd->bqghd', pw, vwb)
        gb = lax.dynamic_slice_in_dim(gate, q0, NSA_Q_BLOCK, axis=1)
        return gb[..., 0:1] * o_c + gb[..., 1:2] * o_s + gb[..., 2:3] * o_w

    o = _sweep(blk, S, NSA_Q_BLOCK)
    return o.reshape(B, S, H * d).astype(q.dtype)


def forgetting_attention(q, k, v, f_logit, f_bias):
    B, S, H, d = q.shape
    pos = jnp.arange(S)
    qf = q.astype(jnp.float32) * (HEAD_DIM ** -0.5)
    kf = k.astype(jnp.float32)
    vf = v.astype(jnp.float32)
    log_f = jax.nn.log_sigmoid(f_logit.astype(jnp.float32) + f_bias.astype(jnp.float32))
    c = jnp.cumsum(log_f, axis=1).transpose(0, 2, 1)

    def blk(i):
        q0 = i * Q_BLOCK
        qb = lax.dynamic_slice_in_dim(qf, q0, Q_BLOCK, axis=1)
        cq = lax.dynamic_slice_in_dim(c, q0, Q_BLOCK, axis=2)
        tq = q0 + jnp.arange(Q_BLOCK)
        mask = pos[None, :] <= tq[:, None]
        s = jnp.einsum('bqhd,bshd->bhqs', qb, kf) + cq[..., None] - c[:, :, None, :]
        p = jax.nn.softmax(jnp.where(mask, s, -jnp.inf), axis=-1)
        return jnp.einsum('bhqs,bshd->bqhd', p, vf)

    o = _sweep(blk, S, Q_BLOCK)
    return o.reshape(B, S, H * d).astype(q.dtype)


def sqrelu_mlp(x, w_up, w_down):
    return jnp.square(jax.nn.relu(x @ w_up)) @ w_down


def setup_inputs(seed: int = 0) -> dict:
    key = jax.random.key(seed)
    ks = jax.random.split(key, 16)

    def nrm(k, shape, scale):
        return jax.random.normal(k, shape, jnp.float32) * scale

    return {
        'x': nrm(ks[0], (BATCH, SEQ, D_MODEL), 1.0),
        'ln_gain': 1.0 + nrm(ks[1], (DEPTH, 2, D_MODEL), 0.02),
        'ln_bias': nrm(ks[2], (DEPTH, 2, D_MODEL), 0.02),
        'mlp_w_up': nrm(ks[3], (DEPTH, D_MODEL, D_FF), D_MODEL ** -0.5),
        'mlp_w_down': nrm(ks[4], (DEPTH, D_FF, D_MODEL), BETA * D_FF ** -0.5),
        'w_in_even': nrm(ks[5], (N_EVEN, D_MODEL, EVEN_COLS), D_MODEL ** -0.5),
        'w_out_even': nrm(ks[6], (N_EVEN, EVEN_MIX, D_MODEL), BETA * EVEN_MIX ** -0.5),
        'diff_lambda': nrm(ks[7], (N_EVEN, 4, HEAD_DIM), 0.1),
        'diff_subln': 1.0 + nrm(ks[8], (N_EVEN, 2 * HEAD_DIM), 0.02),
        'nsa_pe': nrm(ks[9], (N_EVEN, 2, NSA_CMP_LEN, HEAD_DIM), 0.1),
        'nsa_cmp_w1': nrm(ks[10], (N_EVEN, 2, NSA_CMP_LEN * HEAD_DIM, NSA_CMP_HIDDEN), (NSA_CMP_LEN * HEAD_DIM) ** -0.5),
        'nsa_cmp_b1': nrm(ks[11], (N_EVEN, 2, NSA_CMP_HIDDEN), 0.02),
        'nsa_cmp_w2': nrm(ks[12], (N_EVEN, 2, NSA_CMP_HIDDEN, HEAD_DIM), NSA_CMP_HIDDEN ** -0.5),
        'w_in_odd': nrm(ks[13], (N_ODD, D_MODEL, ODD_COLS), D_MODEL ** -0.5),
        'fox_f_bias': jnp.linspace(1.0, 6.0, FOX_HEADS, dtype=jnp.float32)[None, :] + nrm(ks[14], (N_ODD, FOX_HEADS), 0.1),
        'w_out_odd': nrm(ks[15], (N_ODD, FOX_WIDTH, D_MODEL), BETA * FOX_WIDTH ** -0.5),
    }


def reference(x, ln_gain, ln_bias, mlp_w_up, mlp_w_down, w_in_even, w_out_even, diff_lambda, diff_subln,
              nsa_pe, nsa_cmp_w1, nsa_cmp_b1, nsa_cmp_w2, w_in_odd, fox_f_bias, w_out_odd):
    B, S, _ = x.shape
    for layer in range(DEPTH):
        li = layer // 2
        if layer % 2 == 0:
            h = x @ w_in_even[li]
            qa, ka, va, qn, kc, vc, ksl, vsl, kw, vw, g = _split(h, EVEN_SPLITS)
            o_a = diff_attention(qa.reshape(B, S, DIFF_HEADS, 2, HEAD_DIM),
                                 ka.reshape(B, S, DIFF_HEADS, 2, HEAD_DIM),
                                 va.reshape(B, S, DIFF_HEADS, 2 * HEAD_DIM),
                                 diff_lambda[li], diff_subln[li], layer)
            kv = lambda t: t.reshape(B, S, NSA_KV_HEADS, HEAD_DIM)
            o_b = nsa_attention(qn.reshape(B, S, NSA_HEADS, HEAD_DIM), kv(kc), kv(vc), kv(ksl), kv(vsl),
                                kv(kw), kv(vw), g, nsa_pe[li], nsa_cmp_w1[li], nsa_cmp_b1[li], nsa_cmp_w2[li])
            mix = jnp.concatenate([o_a, o_b], axis=-1) @ w_out_even[li]
        else:
            h = x @ w_in_odd[li]
            qf, kf, vf, fl = _split(h, ODD_SPLITS)
            hd = lambda t: t.reshape(B, S, FOX_HEADS, HEAD_DIM)
            mix = forgetting_attention(hd(qf), hd(kf), hd(vf), fl, fox_f_bias[li]) @ w_out_odd[li]
        x = layer_norm(ALPHA * x + mix, ln_gain[layer, 0], ln_bias[layer, 0])
        x = layer_norm(ALPHA * x + sqrelu_mlp(x, mlp_w_up[layer], mlp_w_down[layer]), ln_gain[layer, 1], ln_bias[layer, 1])
    return x
```

```python
import math
from contextlib import ExitStack
import numpy as np
import concourse.bass as bass
import concourse.mybir as mybir
from concourse.bass_utils import run_bass_kernel_spmd

F32 = mybir.dt.float32
BF16 = mybir.dt.bfloat16
AF = mybir.ActivationFunctionType
ALU = mybir.AluOpType

SEQ = 2048
D = 1024
NT = 16
DFF = 4096
DEPTH = 4
ALPHA = (2 * DEPTH) ** 0.25
LN_EPS = 1e-5
NCMP = 127
NEGM = -30000.0
import os
_DBG = os.environ.get('KDBG', '')


class Buf:
    __slots__ = ("w", "r")

    def __init__(self):
        self.w = None
        self.r = {}


class Eng:
    def __init__(self, key, eng, sem):
        self.key = key
        self.eng = eng
        self.sem = sem
        self.count = 0
        self.seen = {}


class Sched:
    def __init__(self, nc, nsem_dma=8):
        self.nc = nc
        self.engs = {}
        self._ctx = []
        for key, eng in (("pe", nc.tensor), ("act", nc.scalar), ("dve", nc.vector), ("pool", nc.gpsimd)):
            cm = nc.semaphore("s_" + key)
            self.engs[key] = Eng(key, eng, cm.__enter__())
            self._ctx.append(cm)
        self.sp = Eng("sp", nc.sync, None)
        self.dsems = []
        for i in range(nsem_dma):
            cm = nc.semaphore(f"d_sp{i}")
            self.dsems.append(cm.__enter__())
            self._ctx.append(cm)
        self.dcnt = [0] * nsem_dma
        self.dn = 0

    def _wait(self, issuer, dep):
        kind, key, val = dep
        k = (kind, id(key) if kind == "d" else key)
        if issuer.seen.get(k, 0) >= val:
            return
        sem = self.engs[key].sem if kind == "e" else key
        issuer.eng.wait_ge(sem, val)
        issuer.seen[k] = val

    @staticmethod
    def _deps(reads, writes):
        deps = []
        for b in reads:
            if b.w is not None:
                deps.append(b.w)
        for b in writes:
            if b.w is not None:
                deps.append(b.w)
            deps.extend(b.r.values())
        return deps

    @staticmethod
    def _mark(reads, writes, tok):
        k = (tok[0], id(tok[1]) if tok[0] == "d" else tok[1])
        for b in reads:
            b.r[k] = tok
        for b in writes:
            b.w = tok
            b.r = {}

    def op(self, ek, fn, reads=(), writes=()):
        e = self.engs[ek]
        for d in self._deps(reads, writes):
            if ek == "pe" and d[0] == "e" and d[1] == "pe":
                continue
            self._wait(e, d)
        ins = fn(e.eng)
        e.count += 1
        ins.then_inc(e.sem, 1)
        self._mark(reads, writes, ("e", ek, e.count))
        return ins

    def dma(self, out, in_, reads=(), writes=(), **kw):
        i = self.dn % len(self.dsems)
        sem = self.dsems[i]
        if self.dcnt[i] > 0:
            self._wait(self.sp, ("d", sem, self.dcnt[i]))
        for d in self._deps(reads, writes):
            self._wait(self.sp, d)
        ins = self.nc.sync.dma_start(out=out, in_=in_, **kw)
        self.dcnt[i] += 16
        ins.then_inc(sem, 16)
        self.dn += 1
        self._mark(reads, writes, ("d", sem, self.dcnt[i]))

    def barrier(self):
        issuers = list(self.engs.values()) + [self.sp]
        for iss in issuers:
            for e2 in self.engs.values():
                if e2.count and e2 is not iss:
                    self._wait(iss, ("e", e2.key, e2.count))
            for i, sem in enumerate(self.dsems):
                if self.dcnt[i]:
                    self._wait(iss, ("d", sem, self.dcnt[i]))

    def close(self):
        for cm in reversed(self._ctx):
            cm.__exit__(None, None, None)


class TT:
    def __init__(self, t, nb=1):
        self.t = t
        self.b = [Buf() for _ in range(nb)]

    def __getitem__(self, k):
        return self.t[k]


def _consts():
    c = {}
    c["c_ident"] = np.eye(128, dtype=np.float32)
    half = 8
    inv = (np.float32(500000.0) ** (-np.arange(half, dtype=np.float32) / np.float32(half))).astype(np.float32)

    def tables(pos):
        pos = pos.astype(np.float32)
        ang = pos[None, :] * inv[:, None]
        C = np.ones((128, pos.shape[0]), np.float32)
        Sn = np.zeros((128, pos.shape[0]), np.float32)
        for base in (0, 64):
            C[base:base + 8] = np.cos(ang)
            C[base + 8:base + 16] = np.cos(ang)
            Sn[base:base + 8] = -np.sin(ang)
            Sn[base + 8:base + 16] = np.sin(ang)
        return C, Sn

    c["c_ropec"], c["c_ropes"] = tables(np.arange(SEQ))
    cc, sc = tables(np.arange(NCMP) * 16 + 31)
    c["c_ropecc"] = np.concatenate([cc, np.ones((128, 1), np.float32)], 1)
    c["c_ropesc"] = np.concatenate([sc, np.zeros((128, 1), np.float32)], 1)
    n = np.arange(128)
    t = np.arange(SEQ)
    valid = ((16 * n[:, None] + 31) <= t[None, :]) & (n[:, None] < NCMP)
    c["c_validc"] = valid.astype(np.float32)
    cs = np.arange(NCMP) * 16
    ce = cs + 31
    ss = np.arange(32) * 64
    ov = ((cs[:, None] <= ss[None, :] + 63) & (ce[:, None] >= ss[None, :])).astype(np.float32)
    ovl = np.zeros((128, 33), np.float32)
    ovl[:NCMP, :32] = ov
    ovl[:NCMP, 32] = 1.0
    c["c_ovl1"] = ovl
    cur = t // 64
    jb = np.arange(32)
    forced = (jb[None, :] == 0) | (jb[None, :] == cur[:, None]) | (jb[None, :] == cur[:, None] - 1)
    future = jb[None, :] > cur[:, None]
    keep = (~forced & ~future).astype(np.float32)
    addc = np.where(forced, 1e4, np.where(future, -1e4, 0.0)).astype(np.float32)
    c["c_keep"] = keep.reshape(NT, 128, 32).transpose(1, 0, 2).reshape(128, NT * 32).copy()
    c["c_addc"] = addc.reshape(NT, 128, 32).transpose(1, 0, 2).reshape(128, NT * 32).copy()
    ee = np.zeros((128, NT, 128), np.float32)
    for kt in range(NT):
        for k in range(128):
            ee[2 * kt + k // 64, kt, k] = 1.0
    c["c_eexp"] = ee.reshape(128, NT * 128)
    es = np.zeros((128, 16, 128), np.float32)
    for h in range(16):
        for r0 in (0, 16, 32, 64, 80, 96):
            es[r0 + h, h, :] = 1.0
    c["c_esel"] = es.reshape(128, 16 * 128)
    return c


_EVEN_OFF = dict(qa=0, ka=512, va=1024, qn=1536, kc=2048, vc=2176, ksl=2304, vsl=2432, kw=2560, vw=2688, g=2816)


def _rope_tiles_even():
    tiles = []
    for h in range(4):
        tiles.append(list(range(_EVEN_OFF["qa"] + 128 * h, _EVEN_OFF["qa"] + 128 * h + 128)))
    for h in range(4):
        tiles.append(list(range(_EVEN_OFF["ka"] + 128 * h, _EVEN_OFF["ka"] + 128 * h + 128)))
    for i in range(4):
        a = _EVEN_OFF["qn"] + 64 * i
        b = _EVEN_OFF["qn"] + 64 * (4 + i)
        tiles.append(list(range(a, a + 64)) + list(range(b, b + 64)))
    tiles.append(list(range(_EVEN_OFF["ksl"], _EVEN_OFF["ksl"] + 128)))
    tiles.append(list(range(_EVEN_OFF["kw"], _EVEN_OFF["kw"] + 128)))
    return tiles


def _host_layout(inp):
    w = inp["w_in_even"]
    tiles = _rope_tiles_even()
    sw = np.zeros((w.shape[0], D, len(tiles) * 128), np.float32)
    main = np.zeros((w.shape[0], D, len(tiles) * 128), np.float32)
    for ti, cols in enumerate(tiles):
        cols = np.array(cols)
        main[:, :, ti * 128:(ti + 1) * 128] = w[:, :, cols]
        for hh in (0, 64):
            sw[:, :, ti * 128 + hh:ti * 128 + hh + 8] = w[:, :, cols[hh + 8:hh + 16]]
            sw[:, :, ti * 128 + hh + 8:ti * 128 + hh + 16] = w[:, :, cols[hh:hh + 8]]
    w2 = inp["nsa_cmp_w2"][:, 0]
    w2sw = np.zeros_like(w2)
    w2sw[:, :, 0:8] = w2[:, :, 8:16]
    w2sw[:, :, 8:16] = w2[:, :, 0:8]
    return dict(w_even_rope=main, w_even_sw=sw, w2_sw=w2sw)


class Prog:
    def __init__(self, n_seq, layers, dbg=False):
        self.n_seq = n_seq
        self.layers = layers
        nc = bass.Bass("TRN2", target_bir_lowering=False)
        self.nc = nc
        self.S = Sched(nc)
        self.d = {}
        self.ucount = 0

    def din(self, name, shape):
        self.d[name] = self.nc.dram_tensor(name, list(shape), F32, kind="ExternalInput").ap()

    def mm(self, out, lhsT, rhs, start, stop, r, w):
        self.S.op("pe", lambda e: e.matmul(out, lhsT=lhsT, rhs=rhs, start=start, stop=stop), reads=r, writes=w)

    def tr(self, out, in_, ident, r, w):
        self.S.op("pe", lambda e: e.transpose(out=out, in_=in_, identity=ident), reads=r, writes=w)

    def act(self, out, in_, func, r, w, bias=None, scale=1.0):
        if bias is None:
            self.S.op("act", lambda e: e.activation(out=out, in_=in_, func=func, scale=scale), reads=r, writes=w)
        else:
            self.S.op("act", lambda e: e.activation(out=out, in_=in_, func=func, bias=bias, scale=scale), reads=r, writes=w)

    def alloc(self, st, name, shape, dt, nb=1):
        self.uid = getattr(self, "uid", 0) + 1
        return TT(st.enter_context(self.nc.sbuf_tensor(f"{name}_{self.uid}", list(shape), dt)), nb)

    def palloc(self, st, name, shape, dt, nb=1):
        return TT(st.enter_context(self.nc.psum_tensor(name, list(shape), dt)), nb)

    def load_w(self, dst_ap, dst_bufs, src_ap, K, n, eng="pool"):
        i = self.stg_i % 2
        self.stg_i += 1
        stg = self.stg[i]
        if K is None:
            v = stg.t[:, 0:n]
        else:
            v = stg.t[:, 0:K * n].rearrange("p (k n) -> p k n", k=K)
        self.S.dma(v, src_ap, writes=stg.b)
        if eng == "act":
            self.S.op(eng, lambda e: e.copy(out=dst_ap, in_=v), reads=stg.b, writes=dst_bufs)
        else:
            self.S.op(eng, lambda e: e.tensor_copy(out=dst_ap, in_=v), reads=stg.b, writes=dst_bufs)

    def load_w_part(self, dst_ap, dst_bufs, src_ap, P0, P1, K, n, eng="pool"):
        i = self.stg_i % 2
        self.stg_i += 1
        stg = self.stg[i]
        v = stg.t[P0:P1, 0:K * n].rearrange("p (k n) -> p k n", k=K)
        self.S.dma(v, src_ap, writes=stg.b)
        if eng == "act":
            self.S.op(eng, lambda e: e.copy(out=dst_ap, in_=v), reads=stg.b, writes=dst_bufs)
        else:
            self.S.op(eng, lambda e: e.tensor_copy(out=dst_ap, in_=v), reads=stg.b, writes=dst_bufs)

    def load_const_bf(self, dst, name, P, n):
        for c0 in range(0, n, 4096):
            w = min(4096, n - c0)
            i = self.stg_i % 2
            self.stg_i += 1
            stg = self.stg[i]
            self.S.dma(stg.t[0:P, 0:w], self.d[name][0:P, c0:c0 + w], writes=stg.b)
            self.S.op("pool", lambda e: e.tensor_copy(out=dst.t[0:P, c0:c0 + w], in_=stg.t[0:P, 0:w]), reads=stg.b, writes=dst.b)

    def build(self):
        nc, S = self.nc, self.S
        ns = self.n_seq
        self.din("x", [ns, SEQ, D])
        self.din("ln_gain", [DEPTH, 2, D])
        self.din("ln_bias", [DEPTH, 2, D])
        self.din("mlp_w_up", [DEPTH, D, DFF])
        self.din("mlp_w_down", [DEPTH, DFF, D])
        self.din("w_in_even", [2, D, 2840])
        self.din("w_even_rope", [2, D, 14 * 128])
        self.din("w_even_sw", [2, D, 14 * 128])
        self.din("w_out_even", [2, D, D])
        self.din("diff_lambda", [2, 4, 64])
        self.din("diff_subln", [2, 128])
        self.din("nsa_pe", [2, 2, 32, 64])
        self.din("nsa_cmp_w1", [2, 2, 2048, 256])
        self.din("nsa_cmp_b1", [2, 2, 256])
        self.din("nsa_cmp_w2", [2, 2, 256, 64])
        self.din("w2_sw", [2, 256, 64])
        self.din("w_in_odd", [2, D, 3088])
        self.din("fox_f_bias", [2, 16])
        self.din("w_out_odd", [2, D, D])
        for k, v in _consts().items():
            self.din(k, v.shape)
        self.out = nc.dram_tensor("out", [ns, SEQ, D], F32, kind="ExternalOutput").ap()
        self.xscr = nc.dram_tensor("xscr", [SEQ, D], F32, kind="Internal").ap()

        with ExitStack() as st:
            self.xT = self.alloc(st, "xT", [128, 8, SEQ], BF16, nb=NT)
            self.stg = [self.alloc(st, f"stg{i}", [128, 4096], F32) for i in range(2)]
            self.stg_i = 0
            self.identb = self.alloc(st, "identb", [128, 128], BF16)
            self.identf = self.alloc(st, "identf", [128, 128], F32)
            self.onesf = self.alloc(st, "onesf", [128, 512], F32)
            self.onesb = self.alloc(st, "onesb", [128, 128], BF16)
            self.cst = self.alloc(st, "cst", [128, 8], F32)
            self.ps = [self.palloc(st, f"ps{i}", [128, 512], F32) for i in range(7)]
            self.psb = self.palloc(st, "psb", [128, 1024], BF16)
            S.dma(self.identf.t[:], self.d["c_ident"][:, :], writes=self.identf.b)
            S.op("pool", lambda e: e.tensor_copy(out=self.identb.t[:], in_=self.identf.t[:]), reads=self.identf.b, writes=self.identb.b)
            S.op("pool", lambda e: e.memset(self.onesf.t[:], 1.0), writes=self.onesf.b)
            S.op("pool", lambda e: e.memset(self.onesb.t[:], 1.0), writes=self.onesb.b)
            S.op("pool", lambda e: e.memset(self.cst.t[:, 0:1], 1.0), writes=self.cst.b)
            S.op("pool", lambda e: e.memset(self.cst.t[:, 1:2], LN_EPS), writes=self.cst.b)
            S.op("pool", lambda e: e.memset(self.cst.t[:, 2:3], 0.0), writes=self.cst.b)
            S.op("pool", lambda e: e.memset(self.cst.t[:, 3:4], 1e-30), writes=self.cst.b)

            for s in range(ns):
                self.phase_init(s)
                for li, layer in enumerate(self.layers):
                    last = li == len(self.layers) - 1
                    if layer % 2 == 0:
                        self.phase_att_even(layer // 2)
                    else:
                        self.phase_att_odd(layer // 2)
                    self.phase_post(s, layer, last)
            S.barrier()
        S.close()
        return nc

    def emit_xT_tile(self, t, xb):
        S = self.S
        for k in range(8):
            self.tr(self.psb.t[:, k * 128:(k + 1) * 128], xb.t[:, k * 128:(k + 1) * 128], self.identb.t[:],
                    r=xb.b + self.identb.b, w=self.psb.b)
        dst = self.xT.t[:, :, t * 128:(t + 1) * 128]
        src = self.psb.t[:, :].rearrange("p (k n) -> p k n", k=8)
        if t % 2 == 0:
            S.op("dve", lambda e: e.tensor_copy(out=dst, in_=src), reads=self.psb.b, writes=[self.xT.b[t]])
        else:
            S.op("act", lambda e: e.copy(out=dst, in_=src), reads=self.psb.b, writes=[self.xT.b[t]])

    def phase_init(self, s):
        S = self.S
        with ExitStack() as st:
            xin = [self.alloc(st, f"xin{i}", [128, D], F32) for i in range(4)]
            xsc = [self.alloc(st, f"xsc{i}", [128, D], F32) for i in range(4)]
            xb = [self.alloc(st, f"xb{i}", [128, D], BF16) for i in range(4)]
            for t in range(NT):
                a, c, b = xin[t % 4], xsc[t % 4], xb[t % 4]
                S.dma(a.t[:], self.d["x"][s, t * 128:(t + 1) * 128, :], writes=a.b)
                S.op("act", lambda e: e.mul(out=c.t[:], in_=a.t[:], mul=float(ALPHA)), reads=a.b, writes=c.b)
                S.dma(self.xscr[t * 128:(t + 1) * 128, :], c.t[:], reads=c.b)
                S.op("dve", lambda e: e.tensor_copy(out=b.t[:], in_=a.t[:]), reads=a.b, writes=b.b)
                self.emit_xT_tile(t, b)
            S.barrier()

    def layernorm_tile(self, src_ap, src_bufs, gb, bb, tmp, y):
        S = self.S
        st6, mv, rs = tmp
        for i in range(2):
            S.op("dve", lambda e: e.bn_stats(out=st6.t[:, i * 6:(i + 1) * 6], in_=src_ap[:, i * 512:(i + 1) * 512]), reads=src_bufs, writes=st6.b)
        S.op("dve", lambda e: e.bn_aggr(out=mv.t[:, 0:2], in_=st6.t[:, 0:12]), reads=st6.b, writes=mv.b)
        self.act(rs.t[:, 0:1], mv.t[:, 1:2], AF.Sqrt, r=mv.b + self.cst.b, w=rs.b, bias=self.cst.t[:, 1:2], scale=1.0)
        S.op("dve", lambda e: e.reciprocal(out=rs.t[:, 0:1], in_=rs.t[:, 0:1]), reads=rs.b, writes=rs.b)
        S.op("dve", lambda e: e.tensor_scalar(out=y.t[:], in0=src_ap, scalar1=mv.t[:, 0:1], scalar2=rs.t[:, 0:1], op0=ALU.subtract, op1=ALU.mult), reads=src_bufs + mv.b + rs.b, writes=y.b)
        S.op("pool", lambda e: e.tensor_tensor(out=y.t[:], in0=y.t[:], in1=gb.t[:], op=ALU.mult), reads=y.b + gb.b, writes=y.b)
        S.op("dve", lambda e: e.tensor_tensor(out=y.t[:], in0=y.t[:], in1=bb.t[:], op=ALU.add), reads=y.b + bb.b, writes=y.b)

    def ln_staged(self, xres, gb, bb, lnb, post_fn):
        S = self.S
        st6s, mvall, rsall = lnb
        for t in range(NT):
            st6 = st6s[t % 4]
            src = xres.t[:, t, :]
            for i in range(2):
                S.op("dve", lambda e: e.bn_stats(out=st6.t[:, i * 6:(i + 1) * 6], in_=src[:, i * 512:(i + 1) * 512]), reads=[xres.b[t]], writes=st6.b)
            S.op("dve", lambda e: e.bn_aggr(out=mvall.t[:, t, 0:2], in_=st6.t[:, 0:12]), reads=st6.b, writes=mvall.b)
        self.act(rsall.t[:, 0:NT], mvall.t[:, :, 1], AF.Sqrt, r=mvall.b + self.cst.b, w=rsall.b, bias=self.cst.t[:, 1:2], scale=1.0)
        S.op("dve", lambda e: e.reciprocal(out=rsall.t[:, 0:NT], in_=rsall.t[:, 0:NT]), reads=rsall.b, writes=rsall.b)
        SK = 3
        for i in range(NT + SK):
            if i < NT:
                t = i
                src = xres.t[:, t, :]
                S.op("dve", lambda e: e.tensor_scalar(out=src, in0=src, scalar1=mvall.t[:, t, 0:1], scalar2=rsall.t[:, t:t + 1], op0=ALU.subtract, op1=ALU.mult),
                     reads=[xres.b[t]] + mvall.b + rsall.b, writes=[xres.b[t]])
                S.op("pool", lambda e: e.tensor_tensor(out=src, in0=src, in1=gb.t[:], op=ALU.mult), reads=[xres.b[t]] + gb.b, writes=[xres.b[t]])
            if i >= SK:
                t = i - SK
                src = xres.t[:, t, :]
                S.op("dve", lambda e: e.tensor_tensor(out=src, in0=src, in1=bb.t[:], op=ALU.add), reads=[xres.b[t]] + bb.b, writes=[xres.b[t]])
                post_fn(t)

    def phase_post(self, s, layer, last):
        S = self.S
        li = layer // 2
        wout = self.d["w_out_even" if layer % 2 == 0 else "w_out_odd"][li]
        with ExitStack() as st:
            xres = self.alloc(st, "xres", [128, NT, D], F32, nb=NT)
            gb = [self.alloc(st, f"gb{i}", [128, D], F32) for i in range(2)]
            bb = [self.alloc(st, f"bb{i}", [128, D], F32) for i in range(2)]
            lnb = ([self.alloc(st, f"st6{i}", [128, 12], F32) for i in range(4)], self.alloc(st, "mvall", [128, NT, 2], F32), self.alloc(st, "rsall", [128, NT], F32))
            xbt = [self.alloc(st, f"xbt{i}", [128, D], BF16) for i in range(4)]
            wup = [self.alloc(st, f"wup{i}", [128, 8, 512], BF16) for i in range(2)]
            wdn = [self.alloc(st, f"wdn{i}", [128, 4, D], BF16) for i in range(2)]
            wup_d = self.d["mlp_w_up"][layer].rearrange("(k p) c -> p k c", p=128)
            wdn_d = self.d["mlp_w_down"][layer].rearrange("(k p) c -> p k c", p=128)

            def load_slice(sl):
                wu, wd = wup[sl % 2], wdn[sl % 2]
                self.load_w(wu.t[:, :, :], wu.b, wup_d[:, :, sl * 512:(sl + 1) * 512], 8, 512)
                self.load_w(wd.t[:, :, :], wd.b, wdn_d[:, sl * 4:(sl + 1) * 4, :], 4, D)
            with ExitStack() as st2:
                wo = self.alloc(st2, "wo", [128, 8, D], BF16, nb=2)
                for hf in range(2):
                    self.load_w(wo.t[:, :, hf * 512:(hf + 1) * 512], [wo.b[hf]],
                                wout.rearrange("(k p) c -> p k c", p=128)[:, :, hf * 512:(hf + 1) * 512], 8, 512, eng=("act" if hf == 0 else "dve"))
                for t in range(NT):
                    S.dma(xres.t[:, t, :], self.xscr[t * 128:(t + 1) * 128, :], writes=[xres.b[t]])
                for i in range(2):
                    S.dma(gb[i].t[:], self.d["ln_gain"][layer, i:i + 1, :].partition_broadcast(128), writes=gb[i].b)
                    S.dma(bb[i].t[:], self.d["ln_bias"][layer, i:i + 1, :].partition_broadcast(128), writes=bb[i].b)
                load_slice(0)
                n = 0
                for t in range(NT):
                    for hf in range(2):
                        pb = self.ps[3 + n % 4]
                        n += 1
                        for k in range(8):
                            self.mm(pb.t[:, :], self.xT.t[:, k, t * 128:(t + 1) * 128], wo.t[:, k, hf * 512:(hf + 1) * 512],
                                    k == 0, k == 7, r=[self.xT.b[t], wo.b[hf]], w=pb.b)
                        dst = xres.t[:, t, hf * 512:(hf + 1) * 512]
                        S.op("dve", lambda e: e.tensor_tensor(out=dst, in0=pb.t[:, :], in1=dst, op=ALU.add), reads=pb.b + [xres.b[t]], writes=[xres.b[t]])
                S.barrier()
            def post1(t):
                xb_ = xbt[t % 4]
                S.op("act", lambda e: e.copy(out=xb_.t[:], in_=xres.t[:, t, :]), reads=[xres.b[t]], writes=xb_.b)
                self.emit_xT_tile(t, xb_)
                S.op("act", lambda e: e.mul(out=xres.t[:, t, :], in_=xres.t[:, t, :], mul=float(ALPHA)), reads=[xres.b[t]], writes=[xres.b[t]])
            self.ln_staged(xres, gb[0], bb[0], lnb, post1)
            with ExitStack() as st2:
                hT = [self.alloc(st2, f"hT{i}", [128, 4, 512], BF16) for i in range(2)]
                n = 0
                hn = 0
                for sl in range(8):
                    wu, wd = wup[sl % 2], wdn[sl % 2]
                    if sl + 1 < 8:
                        load_slice(sl + 1)
                    for c in range(4):
                        h = hT[hn % 2]
                        hn += 1
                        for f in range(4):
                            pb = self.ps[n % 7]
                            n += 1
                            for k in range(8):
                                self.mm(pb.t[:, :], wu.t[:, k, f * 128:(f + 1) * 128], self.xT.t[:, k, c * 512:(c + 1) * 512],
                                        k == 0, k == 7, r=wu.b + self.xT.b[4 * c:4 * c + 4], w=pb.b)
                            self.act(h.t[:, f, :], pb.t[:, :], AF.Relu, r=pb.b, w=h.b)
                            self.act(h.t[:, f, :], h.t[:, f, :], AF.Square, r=h.b, w=h.b)
                        for tt in range(4):
                            t = 4 * c + tt
                            for hf in range(2):
                                pb = self.ps[n % 7]
                                n += 1
                                for f in range(4):
                                    self.mm(pb.t[:, :], h.t[:, f, tt * 128:(tt + 1) * 128], wd.t[:, f, hf * 512:(hf + 1) * 512],
                                            f == 0, f == 3, r=h.b + wd.b, w=pb.b)
                                dst = xres.t[:, t, hf * 512:(hf + 1) * 512]
                                S.op("dve", lambda e: e.tensor_tensor(out=dst, in0=pb.t[:, :], in1=dst, op=ALU.add), reads=pb.b + [xres.b[t]], writes=[xres.b[t]])
                S.barrier()
            def post2(t):
                if last:
                    S.dma(self.out[s, t * 128:(t + 1) * 128, :], xres.t[:, t, :], reads=[xres.b[t]])
                else:
                    xb_ = xbt[t % 4]
                    S.op("act", lambda e: e.copy(out=xb_.t[:], in_=xres.t[:, t, :]), reads=[xres.b[t]], writes=xb_.b)
                    self.emit_xT_tile(t, xb_)
                    S.op("act", lambda e: e.mul(out=xres.t[:, t, :], in_=xres.t[:, t, :], mul=float(ALPHA)), reads=[xres.b[t]], writes=[xres.b[t]])
                    S.dma(self.xscr[t * 128:(t + 1) * 128, :], xres.t[:, t, :], reads=[xres.b[t]])
            self.ln_staged(xres, gb[1], bb[1], lnb, post2)
            S.barrier()

    def run_units(self, units, look=2):
        n = len(units)
        pending = []
        for i in range(n + look):
            if i < n:
                units[i]["qk"]()
                units[i]["exp"]()
            for (due, f2) in [p_ for p_ in pending if p_[0] <= i]:
                f2()
            pending = [p_ for p_ in pending if p_[0] > i]
            if i >= look:
                units[i - look]["pv"]()
                f = units[i - look].get("fin")
                if f:
                    f()
                f2 = units[i - look].get("fin2")
                if f2:
                    pending.append((i + 3, f2))
        for (due, f2) in pending:
            f2()

    def otm_to_oT(self, otm):
        for t in range(NT):
            tmp = TT(otm.t[:, t, :])
            tmp.b = [otm.b[t]]
            self.emit_xT_tile(t, tmp)

    def causal_fix(self, pT, col0):
        blk = pT.t[:, col0:col0 + 128]
        self.S.op("pool", lambda e: e.affine_select(out=blk, in_=blk, pattern=[[1, 128]], compare_op=ALU.is_ge, fill=0.0,
                                                      base=0, channel_multiplier=-1), reads=pT.b, writes=pT.b)

    def phase_att_odd(self, li):
        S = self.S
        W = self.d["w_in_odd"][li].rearrange("(k p) c -> p k c", p=128)
        with ExitStack() as st:
            otm = self.alloc(st, "otm", [128, NT, D], BF16, nb=NT)
            haug = self.alloc(st, "haug", [128, SEQ], BF16)
            esel = self.alloc(st, "esel", [128, 16 * 128], BF16)
            cneg_tm = self.alloc(st, "cneg_tm", [128, NT * 16], F32)
            self.load_const_bf(esel, "c_esel", 128, 16 * 128)
            with ExitStack() as st2:
                wf = self.alloc(st2, "wf", [128, 8, 16], BF16)
                fb = self.alloc(st2, "fb", [16, 1], F32)
                cneg = self.alloc(st2, "cneg", [16, SEQ], F32)
                q3 = self.alloc(st2, "q3", [16, SEQ], F32)
                tf = self.alloc(st2, "tf", [16, SEQ], F32)
                tb = self.alloc(st2, "tb", [16, SEQ], BF16)
                self.load_w(wf.t[:, :, :], wf.b, W[:, :, 3072:3088], 8, 16)
                S.dma(fb.t[:, :], self.d["fox_f_bias"][li:li + 1, :].rearrange("o h -> h o"), writes=fb.b, allow_slow_non_contiguous=True)
                S.op("dve", lambda e: e.tensor_scalar_mul(out=fb.t[:, :], in0=fb.t[:, :], scalar1=-1.0), reads=fb.b, writes=fb.b)
                S.op("pool", lambda e: e.memset(haug.t[:, :], 0.0), writes=haug.b)
                for c in range(4):
                    pb = self.ps[3 + c % 4]
                    for k in range(8):
                        self.mm(pb.t[0:16, :], wf.t[:, k, :], self.xT.t[:, k, c * 512:(c + 1) * 512], k == 0, k == 7,
                                r=wf.b + self.xT.b[4 * c:4 * c + 4], w=pb.b)
                    self.act(tf.t[:, c * 512:(c + 1) * 512], pb.t[0:16, :], AF.Exp, r=pb.b + fb.b, w=tf.b, bias=fb.t[:, 0:1], scale=-1.0)
                    self.act(tf.t[:, c * 512:(c + 1) * 512], tf.t[:, c * 512:(c + 1) * 512], AF.Ln, r=tf.b + self.cst.b, w=tf.b, bias=self.cst.t[0:16, 0:1], scale=1.0)
                    init = 0.0 if c == 0 else cneg.t[:, c * 512 - 1:c * 512]
                    S.op("dve", lambda e: e.tensor_tensor_scan(out=cneg.t[:, c * 512:(c + 1) * 512], data0=self.onesf.t[0:16, :], data1=tf.t[:, c * 512:(c + 1) * 512],
                                                                 initial=init, op0=ALU.mult, op1=ALU.add), reads=tf.b + self.onesf.b + cneg.b, writes=cneg.b)
                pb = self.ps[3]
                for t in range(NT):
                    self.tr(pb.t[:, t * 16:(t + 1) * 16], cneg.t[:, t * 128:(t + 1) * 128], self.identf.t[0:16, 0:16], r=cneg.b + self.identf.b, w=pb.b)
                S.op("dve", lambda e: e.tensor_copy(out=cneg_tm.t[:, :], in_=pb.t[:, 0:256]), reads=pb.b, writes=cneg_tm.b)
                S.op("dve", lambda e: e.tensor_scalar_mul(out=q3.t[:, :], in0=cneg.t[:, :], scalar1=-8.0), reads=cneg.b, writes=q3.b)
                S.op("act", lambda e: e.copy(out=haug.t[0:16, :], in_=q3.t[:, :]), reads=q3.b, writes=haug.b)
                S.op("dve", lambda e: e.tensor_copy(out=tf.t[:, :], in_=haug.t[0:16, :]), reads=haug.b, writes=tf.b)
                S.op("dve", lambda e: e.tensor_tensor(out=q3.t[:, :], in0=q3.t[:, :], in1=tf.t[:, :], op=ALU.subtract), reads=q3.b + tf.b, writes=q3.b)
                S.op("act", lambda e: e.copy(out=tb.t[:, :], in_=q3.t[:, :]), reads=q3.b, writes=tb.b)
                S.op("act", lambda e: e.copy(out=haug.t[32:48, :], in_=tb.t[:, :]), reads=tb.b, writes=haug.b)
                S.op("dve", lambda e: e.tensor_copy(out=tf.t[:, :], in_=tb.t[:, :]), reads=tb.b, writes=tf.b)
                S.op("dve", lambda e: e.tensor_tensor(out=q3.t[:, :], in0=q3.t[:, :], in1=tf.t[:, :], op=ALU.subtract), reads=q3.b + tf.b, writes=q3.b)
                S.op("act", lambda e: e.copy(out=tb.t[:, :], in_=q3.t[:, :]), reads=q3.b, writes=tb.b)
                S.op("act", lambda e: e.copy(out=haug.t[64:80, :], in_=tb.t[:, :]), reads=tb.b, writes=haug.b)
                S.barrier()
            with ExitStack() as st2:
                wq = [self.alloc(st2, f"wq{i}", [128, 8, 128], BF16) for i in range(2)]
                wk = [self.alloc(st2, f"wk{i}", [128, 8, 128], BF16) for i in range(2)]
                wv = [self.alloc(st2, f"wv{i}", [128, 8, 128], BF16) for i in range(2)]
                qT = [[self.alloc(st2, f"qT{i}{hh}", [128, SEQ], BF16, nb=5) for hh in range(2)] for i in range(2)]
                kT = [[self.alloc(st2, f"kT{i}{hh}", [128, SEQ], BF16, nb=5) for hh in range(2)] for i in range(2)]
                for i in range(2):
                    for hh in range(2):
                        S.op("pool", lambda e: e.memset(qT[i][hh].t[:, :], 0.0), writes=qT[i][hh].b)
                        ab = 64 if hh == 0 else 0
                        for r, src0 in enumerate((0, 32, 64)):
                            S.dma(qT[i][hh].t[ab + 16 * r:ab + 16 * r + 16, :], haug.t[src0:src0 + 16, :], reads=haug.b, writes=[qT[i][hh].b[4]])
                va = [self.alloc(st2, f"va{i}", [128, NT, 2, 66], BF16) for i in range(2)]
                pT = [self.alloc(st2, f"pT{i}", [128, 512], BF16) for i in range(3)]
                rinv = [self.alloc(st2, f"rinv{i}", [128, 4], F32) for i in range(2)]
                for i in range(2):
                    S.op("pool", lambda e: e.memset(va[i].t[:, :, :, 64:66], 1.0), writes=va[i].b)
                pn = 0
                fin_n = 0
                def load_pair(hp):
                    j = hp % 2
                    self.load_w(wq[j].t[:, :, :], wq[j].b, W[:, :, hp * 128:(hp + 1) * 128], 8, 128)
                    self.load_w(wk[j].t[:, :, :], wk[j].b, W[:, :, 1024 + hp * 128:1024 + (hp + 1) * 128], 8, 128)
                    self.load_w(wv[j].t[:, :, :], wv[j].b, W[:, :, 2048 + hp * 128:2048 + (hp + 1) * 128], 8, 128)
                    for hh in range(2):
                        h = hp * 2 + hh
                        sb = 64 if hh == 0 else 0
                        S.op("pool", lambda e: e.tensor_copy(out=kT[j][hh].t[sb:sb + 64, :].rearrange("p (t k) -> p t k", t=NT),
                                                               in_=esel.t[sb:sb + 64, h * 128:(h + 1) * 128].unsqueeze(1).broadcast_to([64, NT, 128])),
                             reads=esel.b, writes=[kT[j][hh].b[4]])
                load_pair(0)
                for hp in range(8):
                    j = hp % 2
                    for (wt, dstT) in ((wq[j], qT[j]), (wk[j], kT[j])):
                        for c in range(4):
                            pb = self.ps[3 + pn % 4]
                            pn += 1
                            for k in range(8):
                                self.mm(pb.t[:, :], wt.t[:, k, :], self.xT.t[:, k, c * 512:(c + 1) * 512], k == 0, k == 7,
                                        r=wt.b + self.xT.b[4 * c:4 * c + 4], w=pb.b)
                            S.op("dve", lambda e: e.tensor_copy(out=dstT[0].t[0:64, c * 512:(c + 1) * 512], in_=pb.t[0:64, :]), reads=pb.b, writes=[dstT[0].b[c]])
                            S.op("dve", lambda e: e.tensor_copy(out=dstT[1].t[64:128, c * 512:(c + 1) * 512], in_=pb.t[64:128, :]), reads=pb.b, writes=[dstT[1].b[c]])
                    for g4 in range(4):
                        pb = self.ps[3 + pn % 4]
                        pn += 1
                        for tt in range(4):
                            t = g4 * 4 + tt
                            for k in range(8):
                                self.mm(pb.t[:, tt * 128:(tt + 1) * 128], self.xT.t[:, k, t * 128:(t + 1) * 128], wv[j].t[:, k, :], k == 0, k == 7,
                                        r=wv[j].b + [self.xT.b[t]], w=pb.b)
                        S.op("dve", lambda e: e.tensor_copy(out=va[j].t[:, g4 * 4:(g4 + 1) * 4, :, 0:64],
                                                              in_=pb.t[:, :].rearrange("p (t h d) -> p t h d", t=4, h=2)), reads=pb.b, writes=va[j].b)
                    if hp + 1 < 8:
                        load_pair(hp + 1)
                    units = []
                    for hh in range(2):
                        h = hp * 2 + hh
                        base = hh * 64
                        for c in range(4):
                            oacc = self.ps[3 + fin_n % 4]
                            ri = rinv[fin_n % 2]
                            fin_n += 1
                            nk = 4 * c + 4
                            for kt in range(nk):
                                u = self.ucount
                                self.ucount += 1
                                sc = self.ps[u % 3]
                                p = pT[u % 3]
                                col0 = 0 if kt < 4 * c else (kt - 4 * c) * 128
                                diag = kt >= 4 * c

                                def qk(sc=sc, kt=kt, c=c, col0=col0, hh=hh, j=j):
                                    self.mm(sc.t[:, col0:512], kT[j][hh].t[:, kt * 128:(kt + 1) * 128],
                                            qT[j][hh].t[:, c * 512 + col0:(c + 1) * 512], True, True,
                                            r=[kT[j][hh].b[kt // 4], kT[j][hh].b[4], qT[j][hh].b[c], qT[j][hh].b[4]], w=sc.b)

                                def ex(sc=sc, p=p, kt=kt, col0=col0, h=h, diag=diag):
                                    self.act(p.t[:, col0:512], sc.t[:, col0:512], AF.Exp, r=sc.b + cneg_tm.b, w=p.b,
                                             bias=cneg_tm.t[:, kt * 16 + h:kt * 16 + h + 1], scale=0.125)
                                    if diag:
                                        self.causal_fix(p, col0)

                                def pv(p=p, kt=kt, c=c, col0=col0, hh=hh, oacc=oacc, j=j):
                                    ov = oacc.t[:, :].rearrange("p (q e) -> p q e", q=4)
                                    for qb in range(col0 // 128, 4):
                                        self.mm(ov[:, qb, 0:65], p.t[:, qb * 128:(qb + 1) * 128], va[j].t[:, kt, hh, 0:65],
                                                kt == 0 and qb == 0, kt == 4 * c + 3 and qb == 3, r=p.b + va[j].b, w=oacc.b)

                                unit = dict(qk=qk, exp=ex, pv=pv)
                                if kt == nk - 1:
                                    def fin(oacc=oacc, ri=ri, c=c, h=h):
                                        ov = oacc.t[:, :].rearrange("p (q e) -> p q e", q=4)
                                        S.op("dve", lambda e: e.reciprocal(out=ri.t[:, :], in_=ov[:, :, 64]), reads=oacc.b, writes=ri.b)
                                        for qb in range(4):
                                            S.op("dve", lambda e: e.tensor_scalar_mul(out=otm.t[:, 4 * c + qb, h * 64:(h + 1) * 64], in0=ov[:, qb, 0:64],
                                                                                    scalar1=ri.t[:, qb:qb + 1]), reads=oacc.b + ri.b, writes=[otm.b[4 * c + qb]])
                                    unit["fin"] = fin
                                units.append(unit)
                    self.run_units(units)
                S.barrier()
            self.otm_to_oT(otm)
            S.barrier()

    def proj_fm(self, wts, dst, evac):
        for c in range(4):
            pbs = []
            for wt in wts:
                pb = self.ps[3 + self.pn % 4]
                self.pn += 1
                for k in range(8):
                    self.mm(pb.t[:, :], wt.t[:, k, :], self.xT.t[:, k, c * 512:(c + 1) * 512], k == 0, k == 7,
                            r=wt.b + self.xT.b[4 * c:4 * c + 4], w=pb.b)
                pbs.append(pb)
            evac(c, pbs)

    def rope_evac(self, parts, ropec, ropes, tmps):
        S = self.S

        def ev(c, pbs):
            t1, t2 = tmps[self.tn % 2]
            self.tn += 1
            sl = slice(c * 512, (c + 1) * 512)
            S.op("dve", lambda e: e.tensor_tensor(out=t1.t[:, :], in0=pbs[0].t[:, :], in1=ropec.t[:, sl], op=ALU.mult), reads=pbs[0].b + ropec.b, writes=t1.b)
            S.op("dve", lambda e: e.tensor_tensor(out=t2.t[:, :], in0=pbs[1].t[:, :], in1=ropes.t[:, sl], op=ALU.mult), reads=pbs[1].b + ropes.b, writes=t2.b)
            for pi, (dst, p0, p1) in enumerate(parts):
                eng = "pool" if (pi + c) % 2 == 0 else "dve"
                S.op(eng, lambda e: e.tensor_tensor(out=dst.t[p0:p1, sl], in0=t1.t[p0:p1, :], in1=t2.t[p0:p1, :], op=ALU.add), reads=t1.b + t2.b, writes=[dst.b[c]])
        return ev

    def plain_evac(self, dst):
        def ev(c, pbs):
            self.S.op("dve", lambda e: e.tensor_copy(out=dst.t[:, c * 512:(c + 1) * 512], in_=pbs[0].t[:, :]), reads=pbs[0].b, writes=[dst.b[c]])
        return ev

    def proj_tm(self, wv, ncols, evac):
        per = max(1, 512 // ncols)
        per = min(per, 4)
        for t0 in range(0, NT, per):
            pb = self.ps[3 + self.pn % 4]
            self.pn += 1
            for tt in range(per):
                t = t0 + tt
                for k in range(8):
                    self.mm(pb.t[:, tt * ncols:(tt + 1) * ncols], self.xT.t[:, k, t * 128:(t + 1) * 128], wv.t[:, k, 0:ncols], k == 0, k == 7,
                            r=wv.b + [self.xT.b[t]], w=pb.b)
            evac(t0, per, pb)

    def phase_att_even(self, li):
        S = self.S
        layer = 2 * li
        Wr = self.d["w_even_rope"][li].rearrange("(k p) c -> p k c", p=128)
        Ws = self.d["w_even_sw"][li].rearrange("(k p) c -> p k c", p=128)
        Wn = self.d["w_in_even"][li].rearrange("(k p) c -> p k c", p=128)
        self.pn = 0
        self.tn = 0
        lam_init = 0.8 - 0.6 * math.exp(-0.3 * layer)
        with ExitStack() as st:
            otm = self.alloc(st, "otm", [128, NT, D], BF16, nb=NT)
            ropec = self.alloc(st, "ropec", [128, SEQ], F32)
            ropes = self.alloc(st, "ropes", [128, SEQ], F32)
            tmps = [(self.alloc(st, f"rt1{i}", [128, 512], F32), self.alloc(st, f"rt2{i}", [128, 512], F32)) for i in range(2)]
            pT = [self.alloc(st, f"pT{i}", [128, 512], BF16) for i in range(3)]
            for c in range(4):
                S.dma(ropec.t[:, c * 512:(c + 1) * 512], self.d["c_ropec"][:, c * 512:(c + 1) * 512], writes=ropec.b)
                S.dma(ropes.t[:, c * 512:(c + 1) * 512], self.d["c_ropes"][:, c * 512:(c + 1) * 512], writes=ropes.b)
            with ExitStack() as st1:
                qn = [self.alloc(st1, f"qn{i}", [128, SEQ], BF16, nb=4) for i in range(8)]
                ksl = [self.alloc(st1, f"ksl{g}", [128, SEQ], BF16, nb=5) for g in range(2)]
                kw = [self.alloc(st1, f"kw{g}", [128, SEQ], BF16, nb=4) for g in range(2)]
                for tq in qn + ksl + kw:
                    S.op("pool", lambda e: e.memset(tq.t[:, :], 0.0), writes=tq.b)
                for g in range(2):
                    m0 = 64 if g == 0 else 0
                    i_ = self.stg_i % 2
                    self.stg_i += 1
                    stg = self.stg[i_]
                    S.dma(stg.t[m0:m0 + 32, 0:NT * 128], self.d["c_eexp"][0:32, :], writes=stg.b)
                    S.op("pool", lambda e: e.tensor_copy(out=ksl[g].t[m0:m0 + 32, :], in_=stg.t[m0:m0 + 32, 0:NT * 128]), reads=stg.b, writes=[ksl[g].b[4]])
                vsl = self.alloc(st1, "vsl", [128, NT, 2, 66], BF16)
                vw = self.alloc(st1, "vw", [128, NT, 2, 66], BF16)
                gates = self.alloc(st1, "gates", [128, NT, 24], F32)
                kcT2 = [self.alloc(st1, f"kcT2{g}", [128, 128], BF16) for g in range(2)]
                RC = self.alloc(st1, "RC", [128, 2, 98], BF16)
                S.op("pool", lambda e: e.memset(vsl.t[:, :, :, 64:66], 1.0), writes=vsl.b)
                S.op("pool", lambda e: e.memset(vw.t[:, :, :, 64:66], 1.0), writes=vw.b)
                S.op("pool", lambda e: e.memset(RC.t[:, :, :], 0.0), writes=RC.b)
                with ExitStack() as st2:
                    wA = [self.alloc(st2, f"wA{i}", [128, 8, 128], BF16) for i in range(2)]
                    wB = [self.alloc(st2, f"wB{i}", [128, 8, 128], BF16) for i in range(2)]
                    cmpT = []
                    for j in range(2):
                        tcm = TT(otm.t[:, 8 + 2 * j:10 + 2 * j, :].rearrange("p a b -> p (a b)"), nb=4)
                        cmpT.append(tcm)
                    w1t = TT(otm.t[:, 0:8, :].rearrange("p a (b c) -> p (a b) c", c=256))
                    hidT = [[self.alloc(st2, f"hidT{j}{g}", [128, 2, 128], BF16) for g in range(2)] for j in range(2)]
                    peT = self.alloc(st2, "peT", [64, 32], BF16)
                    b1t = self.alloc(st2, "b1t", [128, 2], F32)
                    bias2 = self.alloc(st2, "bias2", [128, 2], F32)
                    w2p = self.alloc(st2, "w2p", [128, 2, 192], BF16)
                    w2sp = self.alloc(st2, "w2sp", [128, 2, 192], BF16)
                    w2v = self.alloc(st2, "w2v", [128, 2, 64], BF16)
                    gz = [self.alloc(st2, f"gz{i}", [128, 128], F32) for i in range(3)]
                    ccs = self.alloc(st2, "ccs", [128, 128], F32)
                    scs = self.alloc(st2, "scs", [128, 128], F32)
                    ovl = self.alloc(st2, "ovl", [128, 33], F32)
                    stages = []
                    for ti, parts in [(8, [(qn[0], 0, 64), (qn[4], 64, 128)]), (9, [(qn[1], 0, 64), (qn[5], 64, 128)]),
                                      (10, [(qn[2], 0, 64), (qn[6], 64, 128)]), (11, [(qn[3], 0, 64), (qn[7], 64, 128)]),
                                      (12, [(ksl[0], 0, 64), (ksl[1], 64, 128)]), (13, [(kw[0], 0, 64), (kw[1], 64, 128)])]:
                        def ld(i, ti=ti):
                            a, b = wA[i % 2], wB[i % 2]
                            self.load_w(a.t[:, :, :], a.b, Wr[:, :, ti * 128:(ti + 1) * 128], 8, 128)
                            self.load_w(b.t[:, :, :], b.b, Ws[:, :, ti * 128:(ti + 1) * 128], 8, 128)

                        def cp(i, parts=parts):
                            self.proj_fm([wA[i % 2], wB[i % 2]], None, self.rope_evac(parts, ropec, ropes, tmps))
                        stages.append((ld, cp))
                    for j, off in ((0, _EVEN_OFF["kc"]), (1, _EVEN_OFF["vc"])):
                        def ld(i, off=off):
                            a = wA[i % 2]
                            self.load_w(a.t[:, :, :], a.b, Wn[:, :, off:off + 128], 8, 128)

                        def cp(i, j=j):
                            self.proj_fm([wA[i % 2]], cmpT[j], self.plain_evac(cmpT[j]))
                        stages.append((ld, cp))
                    for off, dstv in ((_EVEN_OFF["vsl"], vsl), (_EVEN_OFF["vw"], vw)):
                        def ld(i, off=off):
                            a = wA[i % 2]
                            self.load_w(a.t[:, :, :], a.b, Wn[:, :, off:off + 128], 8, 128)

                        def cp(i, dstv=dstv):
                            def ev(t0, per, pb, dstv=dstv):
                                S.op("dve", lambda e: e.tensor_copy(out=dstv.t[:, t0:t0 + per, :, 0:64],
                                                                      in_=pb.t[:, :].rearrange("p (t g d) -> p t g d", t=per, g=2)), reads=pb.b, writes=dstv.b)
                            self.proj_tm(wA[i % 2], 128, ev)
                        stages.append((ld, cp))

                    def ldg(i):
                        a = wA[i % 2]
                        self.load_w(a.t[:, :, 0:24], a.b, Wn[:, :, _EVEN_OFF["g"]:_EVEN_OFF["g"] + 24], 8, 24)

                    def cpg(i):
                        def evg(t0, per, pb):
                            self.act(gates.t[:, t0:t0 + per, :], pb.t[:, 0:per * 24].rearrange("p (t g) -> p t g", t=per), AF.Sigmoid, r=pb.b, w=gates.b)
                        self.proj_tm(wA[i % 2], 24, evg)
                    stages.append((ldg, cpg))
                    def load_w1(j):
                        w1d = self.d["nsa_cmp_w1"][li, j].rearrange("(l d) c -> d l c", d=64)
                        for half in range(2):
                            for lh in range(2):
                                self.load_w_part(w1t.t[half * 64:(half + 1) * 64, lh * 16:(lh + 1) * 16, :], w1t.b,
                                                 w1d[:, lh * 16:(lh + 1) * 16, :], half * 64, (half + 1) * 64, 16, 256,
                                                 eng=("dve" if lh == 0 else "act"))
                    load_w1(0)
                    stages[0][0](0)
                    for i, (ld_, cp_) in enumerate(stages):
                        if i + 1 < len(stages):
                            stages[i + 1][0](i + 1)
                        cp_(i)
                    S.dma(ccs.t[:, :], self.d["c_ropecc"][:, :], writes=ccs.b)
                    S.dma(scs.t[:, :], self.d["c_ropesc"][:, :], writes=scs.b)
                    S.dma(ovl.t[:, :], self.d["c_ovl1"][:, :], writes=ovl.b)
                    S.op("pool", lambda e: e.memset(w2p.t[:, :, :], 0.0), writes=w2p.b)
                    S.op("pool", lambda e: e.memset(w2sp.t[:, :, :], 0.0), writes=w2sp.b)
                    self.load_w(w2p.t[:, :, 64:128], w2p.b, self.d["nsa_cmp_w2"][li, 0].rearrange("(m p) c -> p m c", p=128), 2, 64)
                    self.load_w(w2sp.t[:, :, 64:128], w2sp.b, self.d["w2_sw"][li].rearrange("(m p) c -> p m c", p=128), 2, 64)
                    self.load_w(w2v.t[:, :, :], w2v.b, self.d["nsa_cmp_w2"][li, 1].rearrange("(m p) c -> p m c", p=128), 2, 64)
                    for j in range(2):
                        if j == 1:
                            load_w1(1)
                        i = self.stg_i % 2
                        self.stg_i += 1
                        stg = self.stg[i]
                        S.dma(stg.t[0:64, 0:32], self.d["nsa_pe"][li, j].rearrange("l d -> d l"), writes=stg.b, allow_slow_non_contiguous=True)
                        S.op("pool", lambda e: e.tensor_copy(out=peT.t[:, :], in_=stg.t[0:64, 0:32]), reads=stg.b, writes=peT.b)
                        S.dma(b1t.t[:, :], self.d["nsa_cmp_b1"][li, j].rearrange("(m p) -> p m", p=128), writes=b1t.b, allow_slow_non_contiguous=True)
                        pbias = self.ps[3 + self.pn % 4]
                        self.pn += 1
                        for m in range(2):
                            for l in range(32):
                                self.mm(pbias.t[:, m:m + 1], w1t.t[0:64, l, m * 128:(m + 1) * 128], peT.t[0:64, l:l + 1], l == 0, l == 31,
                                        r=w1t.b + peT.b, w=pbias.b)
                        S.op("dve", lambda e: e.tensor_tensor(out=bias2.t[:, :], in0=pbias.t[:, 0:2], in1=b1t.t[:, :], op=ALU.add), reads=pbias.b + b1t.b, writes=bias2.b)
                        for g in range(2):
                            src = cmpT[j].t[:, :].rearrange("p (n s) -> p n s", s=16)
                            for m in range(2):
                                ph = self.ps[3 + self.pn % 4]
                                self.pn += 1
                                for l in range(32):
                                    self.mm(ph.t[:, 0:NCMP], w1t.t[g * 64:(g + 1) * 64, l, m * 128:(m + 1) * 128],
                                            src[g * 64:(g + 1) * 64, l // 16:l // 16 + NCMP, l % 16], l == 0, l == 31,
                                            r=w1t.b + cmpT[j].b, w=ph.b)
                                z, u, sg = gz
                                self.act(z.t[:, 0:NCMP], ph.t[:, 0:NCMP], AF.Identity, r=ph.b + bias2.b, w=z.b, bias=bias2.t[:, m:m + 1], scale=1.0)
                                S.op("pool", lambda e: e.tensor_tensor(out=u.t[:, 0:NCMP], in0=z.t[:, 0:NCMP], in1=z.t[:, 0:NCMP], op=ALU.mult), reads=z.b, writes=u.b)
                                S.op("dve", lambda e: e.tensor_scalar(out=u.t[:, 0:NCMP], in0=u.t[:, 0:NCMP], scalar1=0.044715, scalar2=1.0, op0=ALU.mult, op1=ALU.add), reads=u.b, writes=u.b)
                                S.op("pool", lambda e: e.tensor_tensor(out=u.t[:, 0:NCMP], in0=u.t[:, 0:NCMP], in1=z.t[:, 0:NCMP], op=ALU.mult), reads=u.b + z.b, writes=u.b)
                                self.act(sg.t[:, 0:NCMP], u.t[:, 0:NCMP], AF.Sigmoid, r=u.b, w=sg.b, scale=2.0 * 0.7978845608028654)
                                S.op("dve", lambda e: e.tensor_tensor(out=hidT[j][g].t[:, m, 0:NCMP], in0=z.t[:, 0:NCMP], in1=sg.t[:, 0:NCMP], op=ALU.mult), reads=z.b + sg.b, writes=hidT[j][g].b)
                    pa = self.ps[3 + self.pn % 4]
                    self.pn += 1
                    pb2 = self.ps[3 + self.pn % 4]
                    self.pn += 1
                    for (pp, wt) in ((pa, w2p), (pb2, w2sp)):
                        n = 0
                        for g in range(2):
                            for m in range(2):
                                lhs = wt.t[:, m, 64:192] if g == 0 else wt.t[:, m, 0:128]
                                self.mm(pp.t[:, 0:NCMP], lhs, hidT[0][g].t[:, m, 0:NCMP], n == 0, n == 3, r=wt.b + hidT[0][g].b, w=pp.b)
                                n += 1
                    z, u, sg = gz
                    S.op("dve", lambda e: e.tensor_tensor(out=z.t[:, 0:NCMP], in0=pa.t[:, 0:NCMP], in1=ccs.t[:, 0:NCMP], op=ALU.mult), reads=pa.b + ccs.b, writes=z.b)
                    S.op("dve", lambda e: e.tensor_tensor(out=u.t[:, 0:NCMP], in0=pb2.t[:, 0:NCMP], in1=scs.t[:, 0:NCMP], op=ALU.mult), reads=pb2.b + scs.b, writes=u.b)
                    for g in range(2):
                        S.op("pool", lambda e: e.memset(kcT2[g].t[:, :], 0.0), writes=kcT2[g].b)
                        S.op("pool", lambda e: e.tensor_tensor(out=kcT2[g].t[g * 64:(g + 1) * 64, 0:NCMP], in0=z.t[g * 64:(g + 1) * 64, 0:NCMP],
                                                                 in1=u.t[g * 64:(g + 1) * 64, 0:NCMP], op=ALU.add), reads=z.b + u.b, writes=kcT2[g].b)
                    for g in range(2):
                        pv_ = self.ps[3 + self.pn % 4]
                        self.pn += 1
                        for m in range(2):
                            self.mm(pv_.t[0:NCMP, 0:64], hidT[1][g].t[:, m, 0:NCMP], w2v.t[:, m, :], m == 0, m == 1, r=hidT[1][g].b + w2v.b, w=pv_.b)
                        S.op("dve", lambda e: e.tensor_copy(out=RC.t[0:NCMP, g, 0:64], in_=pv_.t[0:NCMP, 0:64]), reads=pv_.b, writes=RC.b)
                        S.op("pool", lambda e: e.tensor_copy(out=RC.t[:, g, 64:97], in_=ovl.t[:, :]), reads=ovl.b, writes=RC.b)
                    S.barrier()
                with ExitStack() as st2:
                    validc = self.alloc(st2, "validc", [128, SEQ], BF16)
                    keep = self.alloc(st2, "keep", [128, NT * 32], F32)
                    addc = self.alloc(st2, "addc", [128, NT * 32], F32)
                    self.load_const_bf(validc, "c_validc", 128, SEQ)
                    S.op("pool", lambda e: e.tensor_scalar(out=validc.t[:, :], in0=validc.t[:, :], scalar1=-1.0, scalar2=30000.0, op0=ALU.add, op1=ALU.mult), reads=validc.b, writes=validc.b)
                    S.dma(keep.t[:, :], self.d["c_keep"][:, :], writes=keep.b)
                    S.dma(addc.t[:, :], self.d["c_addc"][:, :], writes=addc.b)
                    octmps = [self.alloc(st2, f"octmp{g}", [128, 4, 4, 64], F32) for g in range(2)]
                    imps = [self.alloc(st2, f"imp{g}", [128, 4, 32], F32) for g in range(2)]
                    imp2 = self.alloc(st2, "imp2", [128, 4, 32], F32)
                    mx8 = self.alloc(st2, "mx8", [128, 4, 8], F32)
                    negb = self.alloc(st2, "negb", [128, 4, 32], BF16)
                    negT = [self.alloc(st2, f"negT{g}", [32, 512], BF16) for g in range(2)]
                    osum = [self.alloc(st2, f"osum{i}", [128, 4, 64], F32) for i in range(2)]
                    rcp = [self.alloc(st2, f"rcp{i}", [128, 4], F32) for i in range(3)]
                    coef = [self.alloc(st2, f"coef{i}", [128, 4], F32) for i in range(3)]
                    fn = 0
                    for c in range(0 if "nonsamain" in _DBG else 4):
                        cunits = []
                        for g in range(2):
                            octmp = octmps[g]
                            imp = imps[g]
                            units = cunits
                            for i in range(4):
                                h = g * 4 + i
                                u_ = self.ucount
                                self.ucount += 1
                                sc = self.ps[u_ % 3]
                                p = pT[u_ % 3]
                                acc = self.ps[3 + fn % 4]
                                rc_, cf_ = rcp[fn % 3], coef[fn % 3]
                                fn += 1

                                def qk(sc=sc, h=h, g=g, c=c):
                                    self.mm(sc.t[0:NCMP, :], kcT2[g].t[:, 0:NCMP], qn[h].t[:, c * 512:(c + 1) * 512], True, False,
                                            r=kcT2[g].b + [qn[h].b[c]], w=sc.b)
                                    self.mm(sc.t[0:NCMP, :], self.identb.t[0:NCMP, 0:NCMP], validc.t[0:NCMP, c * 512:(c + 1) * 512], False, True,
                                            r=self.identb.b + validc.b, w=sc.b)

                                def ex(sc=sc, p=p, c=c):
                                    self.act(p.t[0:NCMP, :], sc.t[0:NCMP, :], AF.Exp, r=sc.b, w=p.b, scale=0.125)

                                def pv(p=p, acc=acc, g=g):
                                    av = acc.t[:, :].rearrange("p (q e) -> p q e", q=4)
                                    for qb in range(4):
                                        self.mm(av[:, qb, 0:97], p.t[0:NCMP, qb * 128:(qb + 1) * 128], RC.t[0:NCMP, g, 0:97], qb == 0, qb == 3, r=p.b + RC.b, w=acc.b)

                                def fin(acc=acc, rc_=rc_, cf_=cf_, i=i, h=h, c=c, octmp=octmp, imp=imp):
                                    av = acc.t[:, :].rearrange("p (q e) -> p q e", q=4)
                                    S.op("dve", lambda e: e.tensor_scalar_max(out=rc_.t[:, :], in0=av[:, :, 96], scalar1=1e-30), reads=acc.b, writes=rc_.b)
                                    S.op("dve", lambda e: e.reciprocal(out=rc_.t[:, :], in_=rc_.t[:, :]), reads=rc_.b, writes=rc_.b)
                                    S.op("dve", lambda e: e.tensor_tensor(out=cf_.t[:, :], in0=rc_.t[:, :], in1=gates.t[:, 4 * c:4 * c + 4, 3 * h], op=ALU.mult), reads=rc_.b + gates.b, writes=cf_.b)
                                    for qb in range(4):
                                        S.op("dve", lambda e: e.tensor_scalar_mul(out=octmp.t[:, qb, i, :], in0=av[:, qb, 0:64], scalar1=cf_.t[:, qb:qb + 1]), reads=acc.b + cf_.b, writes=octmp.b)
                                        if i == 0:
                                            S.op("dve", lambda e: e.tensor_scalar_mul(out=imp.t[:, qb, :], in0=av[:, qb, 64:96], scalar1=rc_.t[:, qb:qb + 1]), reads=acc.b + rc_.b, writes=imp.b)
                                        else:
                                            S.op("dve", lambda e: e.scalar_tensor_tensor(out=imp.t[:, qb, :], in0=av[:, qb, 64:96], scalar=rc_.t[:, qb:qb + 1], in1=imp.t[:, qb, :],
                                                                                       op0=ALU.mult, op1=ALU.add), reads=acc.b + rc_.b + imp.b, writes=imp.b)
                                units.append(dict(qk=qk, exp=ex, pv=pv, fin=fin))
                        self.run_units(cunits)
                        for g in range(2):
                            imp = imps[g]
                            for qb in range(0 if "notopk" in _DBG else 4):
                                t = 4 * c + qb
                                S.op("dve", lambda e: e.tensor_tensor(out=imp2.t[:, qb, :], in0=imp.t[:, qb, :], in1=keep.t[:, t * 32:(t + 1) * 32], op=ALU.mult), reads=imp.b + keep.b, writes=imp2.b)
                                S.op("dve", lambda e: e.tensor_tensor(out=imp2.t[:, qb, :], in0=imp2.t[:, qb, :], in1=addc.t[:, t * 32:(t + 1) * 32], op=ALU.add), reads=imp2.b + addc.b, writes=imp2.b)
                                S.op("dve", lambda e: e.max(out=mx8.t[:, qb, :], in_=imp2.t[:, qb, :]), reads=imp2.b, writes=mx8.b)
                                nc0 = 64 if g == 0 else 0
                                S.op("dve", lambda e: e.tensor_scalar(out=negb.t[:, qb, :], in0=imp2.t[:, qb, :], scalar1=mx8.t[:, qb, 7:8], scalar2=NEGM, op0=ALU.is_lt, op1=ALU.mult),
                                     reads=imp2.b + mx8.b, writes=negb.b)
                                self.tr(self.psb.t[0:32, qb * 128:(qb + 1) * 128], negb.t[:, qb, :], self.identb.t[:, :], r=negb.b + self.identb.b, w=self.psb.b)
                            if "notopk" not in _DBG:
                                S.op("act", lambda e: e.copy(out=negT[g].t[0:32, :], in_=self.psb.t[0:32, 0:512]), reads=self.psb.b, writes=negT[g].b)
                            for i in range(0 if "notopk" in _DBG else 4):
                                hq = g * 4 + i
                                S.dma(qn[hq].t[nc0:nc0 + 32, c * 512:(c + 1) * 512], negT[g].t[0:32, :], reads=negT[g].b, writes=[qn[hq].b[c]])
                        for g in range(2):
                            octmp = octmps[g]
                            nc0 = 64 if g == 0 else 0
                            units = []
                            for i in range(0 if "noselwin" in _DBG else 4):
                                h = g * 4 + i
                                base = g * 64
                                acc = self.ps[3 + fn % 4]
                                rc_, cf_ = rcp[fn % 3], coef[fn % 3]
                                os_ = osum[i % 2]
                                fn += 1
                                nk = 4 * c + 4
                                for kt in range(nk):
                                    u_ = self.ucount
                                    self.ucount += 1
                                    sc = self.ps[u_ % 3]
                                    p = pT[u_ % 3]
                                    col0 = 0 if kt < 4 * c else (kt - 4 * c) * 128
                                    diag = kt >= 4 * c

                                    def qk(sc=sc, kt=kt, c=c, col0=col0, h=h, g=g):
                                        self.mm(sc.t[:, col0:512], ksl[g].t[:, kt * 128:(kt + 1) * 128], qn[h].t[:, c * 512 + col0:(c + 1) * 512], True, True,
                                                r=[ksl[g].b[kt // 4], ksl[g].b[4], qn[h].b[c]], w=sc.b)

                                    def ex(sc=sc, p=p, col0=col0, diag=diag):
                                        self.act(p.t[:, col0:512], sc.t[:, col0:512], AF.Exp, r=sc.b, w=p.b, scale=0.125)
                                        if diag:
                                            self.causal_fix(p, col0)

                                    def pv(p=p, kt=kt, c=c, col0=col0, acc=acc, g=g):
                                        av = acc.t[:, :].rearrange("p (q e) -> p q e", q=4)
                                        for qb in range(col0 // 128, 4):
                                            self.mm(av[:, qb, 0:65], p.t[:, qb * 128:(qb + 1) * 128], vsl.t[:, kt, g, 0:65],
                                                    kt == 0 and qb == 0, kt == 4 * c + 3 and qb == 3, r=p.b + vsl.b, w=acc.b)
                                    unit = dict(qk=qk, exp=ex, pv=pv)
                                    if kt == nk - 1:
                                        def fin(acc=acc, rc_=rc_, cf_=cf_, i=i, h=h, c=c, os_=os_):
                                            av = acc.t[:, :].rearrange("p (q e) -> p q e", q=4)
                                            S.op("dve", lambda e: e.reciprocal(out=rc_.t[:, :], in_=av[:, :, 64]), reads=acc.b, writes=rc_.b)
                                            S.op("dve", lambda e: e.tensor_tensor(out=cf_.t[:, :], in0=rc_.t[:, :], in1=gates.t[:, 4 * c:4 * c + 4, 3 * h + 1], op=ALU.mult), reads=rc_.b + gates.b, writes=cf_.b)
                                            for qb in range(4):
                                                S.op("dve", lambda e: e.scalar_tensor_tensor(out=os_.t[:, qb, :], in0=av[:, qb, 0:64], scalar=cf_.t[:, qb:qb + 1], in1=octmp.t[:, qb, i, :],
                                                                                           op0=ALU.mult, op1=ALU.add), reads=acc.b + cf_.b + octmp.b, writes=os_.b)
                                        unit["fin"] = fin
                                    units.append(unit)
                                acc = self.ps[3 + fn % 4]
                                rc_, cf_ = rcp[fn % 3], coef[fn % 3]
                                fn += 1
                                kts = list(range(max(0, 4 * c - 4), 4 * c + 4))
                                first = True
                                for kt in kts:
                                    m = kt - (4 * c - 4)
                                    i_lo, i_hi = max(0, m - 4), min(3, m)
                                    u_ = self.ucount
                                    self.ucount += 1
                                    sc = self.ps[u_ % 3]
                                    p = pT[u_ % 3]
                                    c0, c1 = i_lo * 128, (i_hi + 1) * 128

                                    def qk(sc=sc, kt=kt, c=c, c0=c0, c1=c1, h=h, g=g):
                                        self.mm(sc.t[:, c0:c1], kw[g].t[:, kt * 128:(kt + 1) * 128], qn[h].t[:, c * 512 + c0:c * 512 + c1], True, True,
                                                r=[kw[g].b[kt // 4], qn[h].b[c]], w=sc.b)

                                    def ex(sc=sc, p=p, c0=c0, c1=c1, m=m, i_lo=i_lo, i_hi=i_hi):
                                        self.act(p.t[:, c0:c1], sc.t[:, c0:c1], AF.Exp, r=sc.b, w=p.b, scale=0.125)
                                        if m >= 4:
                                            self.causal_fix(p, i_lo * 128)
                                        if m <= 3:
                                            blk = p.t[:, i_hi * 128:(i_hi + 1) * 128]
                                            S.op("pool", lambda e: e.affine_select(out=blk, in_=blk, pattern=[[-1, 128]], compare_op=ALU.is_gt, fill=0.0,
                                                                                    base=0, channel_multiplier=1), reads=p.b, writes=p.b)

                                    def pv(p=p, kt=kt, c=c, i_lo=i_lo, i_hi=i_hi, acc=acc, g=g, first=first):
                                        av = acc.t[:, :].rearrange("p (q e) -> p q e", q=4)
                                        for qb in range(i_lo, i_hi + 1):
                                            self.mm(av[:, qb, 0:65], p.t[:, qb * 128:(qb + 1) * 128], vw.t[:, kt, g, 0:65],
                                                    first and qb == i_lo, kt == 4 * c + 3 and qb == 3, r=p.b + vw.b, w=acc.b)
                                    first = False
                                    unit = dict(qk=qk, exp=ex, pv=pv)
                                    if kt == kts[-1]:
                                        def fin(acc=acc, rc_=rc_, cf_=cf_, i=i, h=h, c=c, os_=os_):
                                            av = acc.t[:, :].rearrange("p (q e) -> p q e", q=4)
                                            S.op("dve", lambda e: e.reciprocal(out=rc_.t[:, :], in_=av[:, :, 64]), reads=acc.b, writes=rc_.b)
                                            S.op("dve", lambda e: e.tensor_tensor(out=cf_.t[:, :], in0=rc_.t[:, :], in1=gates.t[:, 4 * c:4 * c + 4, 3 * h + 2], op=ALU.mult), reads=rc_.b + gates.b, writes=cf_.b)
                                            for qb in range(4):
                                                S.op("dve", lambda e: e.scalar_tensor_tensor(out=otm.t[:, 4 * c + qb, 512 + h * 64:512 + (h + 1) * 64], in0=av[:, qb, 0:64], scalar=cf_.t[:, qb:qb + 1],
                                                                                           in1=os_.t[:, qb, :], op0=ALU.mult, op1=ALU.add), reads=acc.b + cf_.b + os_.b, writes=[otm.b[4 * c + qb]])
                                        unit["fin"] = fin
                                    units.append(unit)
                            self.run_units(units)
                    S.barrier()
            with ExitStack() as st1:
                wA = [self.alloc(st1, f"dwA{i}", [128, 8, 128], BF16) for i in range(4)]
                wB = [self.alloc(st1, f"dwB{i}", [128, 8, 128], BF16) for i in range(4)]
                wV = [self.alloc(st1, f"dwV{i}", [128, 8, 128], BF16) for i in range(2)]
                qa = [self.alloc(st1, f"qa{i}", [128, SEQ], BF16, nb=4) for i in range(2)]
                ka = [[self.alloc(st1, f"ka{i}{cc}", [128, SEQ], BF16, nb=4) for cc in range(2)] for i in range(2)]
                for i in range(2):
                    for cc in range(2):
                        S.op("pool", lambda e: e.memset(ka[i][cc].t[:, :], 0.0), writes=ka[i][cc].b)
                vaa = [self.alloc(st1, f"vaa{i}", [128, NT, 130], BF16) for i in range(2)]
                a0 = self.alloc(st1, "a0", [128, 4, 128], F32)
                ods = [self.alloc(st1, f"od{i}", [128, 4, 128], F32) for i in range(2)]
                junk = self.alloc(st1, "junk", [128, 128], F32)
                sss = [self.alloc(st1, f"ss{i}", [128, 4], F32) for i in range(2)]
                rcp = [self.alloc(st1, f"drcp{i}", [128, 4], F32) for i in range(2)]
                lp = self.alloc(st1, "lp", [128, 256], F32)
                lam = self.alloc(st1, "lam", [128, 4], F32)
                gsub = self.alloc(st1, "gsub", [128, 128], F32)
                for i in range(2):
                    S.op("pool", lambda e: e.memset(vaa[i].t[:, :, 128:130], 1.0), writes=vaa[i].b)
                S.dma(lp.t[:, :], self.d["diff_lambda"][li:li + 1].rearrange("o a d -> o (a d)").partition_broadcast(128), writes=lp.b)
                S.op("dve", lambda e: e.tensor_tensor(out=lp.t[:, 0:64], in0=lp.t[:, 0:64], in1=lp.t[:, 64:128], op=ALU.mult), reads=lp.b, writes=lp.b)
                S.op("dve", lambda e: e.tensor_tensor(out=lp.t[:, 128:192], in0=lp.t[:, 128:192], in1=lp.t[:, 192:256], op=ALU.mult), reads=lp.b, writes=lp.b)
                S.op("dve", lambda e: e.tensor_reduce(out=lam.t[:, 0:1], in_=lp.t[:, 0:64], axis=mybir.AxisListType.X, op=ALU.add), reads=lp.b, writes=lam.b)
                S.op("dve", lambda e: e.tensor_reduce(out=lam.t[:, 1:2], in_=lp.t[:, 128:192], axis=mybir.AxisListType.X, op=ALU.add), reads=lp.b, writes=lam.b)
                self.act(lam.t[:, 0:2], lam.t[:, 0:2], AF.Exp, r=lam.b, w=lam.b)
                S.op("dve", lambda e: e.tensor_tensor(out=lam.t[:, 2:3], in0=lam.t[:, 1:2], in1=lam.t[:, 0:1], op=ALU.subtract), reads=lam.b, writes=lam.b)
                S.op("dve", lambda e: e.tensor_scalar_add(out=lam.t[:, 2:3], in0=lam.t[:, 2:3], scalar1=-float(lam_init)), reads=lam.b, writes=lam.b)
                S.dma(gsub.t[:, :], self.d["diff_subln"][li:li + 1, :].partition_broadcast(128), writes=gsub.b)
                S.op("dve", lambda e: e.tensor_scalar_mul(out=gsub.t[:, :], in0=gsub.t[:, :], scalar1=float(1.0 - lam_init)), reads=gsub.b, writes=gsub.b)
                gn = 0
                def load_head(h):
                    j = h % 2
                    aq, bq, ak, bk, wv_ = wA[2 * j], wB[2 * j], wA[2 * j + 1], wB[2 * j + 1], wV[j]
                    self.load_w(aq.t[:, :, :], aq.b, Wr[:, :, h * 128:(h + 1) * 128], 8, 128)
                    self.load_w(bq.t[:, :, :], bq.b, Ws[:, :, h * 128:(h + 1) * 128], 8, 128)
                    self.load_w(ak.t[:, :, :], ak.b, Wr[:, :, (4 + h) * 128:(5 + h) * 128], 8, 128)
                    self.load_w(bk.t[:, :, :], bk.b, Ws[:, :, (4 + h) * 128:(5 + h) * 128], 8, 128)
                    self.load_w(wv_.t[:, :, :], wv_.b, Wn[:, :, _EVEN_OFF["va"] + h * 128:_EVEN_OFF["va"] + (h + 1) * 128], 8, 128)
                if "nodiff" not in _DBG:
                    load_head(0)
                for h in range(0 if "nodiff" in _DBG else 4):
                    j = h % 2
                    aq, bq, ak, bk, wv_ = wA[2 * j], wB[2 * j], wA[2 * j + 1], wB[2 * j + 1], wV[j]
                    self.proj_fm([aq, bq], None, self.rope_evac([(qa[j], 0, 128)], ropec, ropes, tmps))
                    self.proj_fm([ak, bk], None, self.rope_evac([(ka[j][0], 0, 64), (ka[j][1], 64, 128)], ropec, ropes, tmps))

                    def evv(t0, per, pb, j=j):
                        S.op("dve", lambda e: e.tensor_copy(out=vaa[j].t[:, t0:t0 + per, 0:128], in_=pb.t[:, :].rearrange("p (t d) -> p t d", t=per)), reads=pb.b, writes=vaa[j].b)
                    self.proj_tm(wv_, 128, evv)
                    if h + 1 < 4:
                        load_head(h + 1)
                    units = []
                    for c in range(4):
                        for cc in range(2):
                            base = cc * 64
                            accs = (self.ps[3], self.ps[4]) if gn % 2 == 0 else (self.ps[5], self.ps[6])
                            rc_ = rcp[gn % 2]
                            od, ss = ods[(gn // 2) % 2], sss[(gn // 2) % 2]
                            gn += 1
                            nk = 4 * c + 4
                            for kt in range(nk):
                                u_ = self.ucount
                                self.ucount += 1
                                sc = self.ps[u_ % 3]
                                p = pT[u_ % 3]
                                col0 = 0 if kt < 4 * c else (kt - 4 * c) * 128
                                diag = kt >= 4 * c

                                def qk(sc=sc, kt=kt, c=c, col0=col0, cc=cc, j=j):
                                    self.mm(sc.t[:, col0:512], ka[j][cc].t[:, kt * 128:(kt + 1) * 128], qa[j].t[:, c * 512 + col0:(c + 1) * 512], True, True,
                                            r=[ka[j][cc].b[kt // 4], qa[j].b[c]], w=sc.b)

                                def ex(sc=sc, p=p, col0=col0, diag=diag):
                                    self.act(p.t[:, col0:512], sc.t[:, col0:512], AF.Exp, r=sc.b, w=p.b, scale=0.125)
                                    if diag:
                                        self.causal_fix(p, col0)

                                def pv(p=p, kt=kt, c=c, col0=col0, accs=accs, j=j):
                                    for qb in range(col0 // 128, 4):
                                        bank = accs[qb // 2]
                                        av = bank.t[:, :].rearrange("p (q e) -> p q e", q=2)
                                        first = kt == 0 and qb % 2 == 0
                                        lastm = (kt == 4 * c + qb) and qb % 2 == 1
                                        self.mm(av[:, qb % 2, 0:129], p.t[:, qb * 128:(qb + 1) * 128], vaa[j].t[:, kt, 0:129], first, lastm, r=p.b + vaa[j].b, w=bank.b)
                                unit = dict(qk=qk, exp=ex, pv=pv)
                                if kt == nk - 1:
                                    def fin(accs=accs, rc_=rc_, cc=cc, c=c, h=h, od=od, ss=ss):
                                        for qb in range(4):
                                            bank = accs[qb // 2]
                                            av = bank.t[:, :].rearrange("p (q e) -> p q e", q=2)
                                            S.op("dve", lambda e: e.reciprocal(out=rc_.t[:, qb:qb + 1], in_=av[:, qb % 2, 128:129]), reads=bank.b, writes=rc_.b)
                                            if cc == 0:
                                                S.op("dve", lambda e: e.tensor_scalar_mul(out=a0.t[:, qb, :], in0=av[:, qb % 2, 0:128], scalar1=rc_.t[:, qb:qb + 1]), reads=bank.b + rc_.b, writes=a0.b)
                                            else:
                                                S.op("dve", lambda e: e.tensor_scalar_mul(out=od.t[:, qb, :], in0=av[:, qb % 2, 0:128], scalar1=rc_.t[:, qb:qb + 1]), reads=bank.b + rc_.b, writes=od.b)
                                        if cc == 1:
                                            for qb in range(4):
                                                S.op("dve", lambda e: e.scalar_tensor_tensor(out=od.t[:, qb, :], in0=od.t[:, qb, :], scalar=lam.t[:, 2:3], in1=a0.t[:, qb, :],
                                                                                           op0=ALU.mult, op1=ALU.add), reads=od.b + lam.b + a0.b, writes=od.b)
                                                S.op("dve", lambda e: e.tensor_tensor(out=junk.t[:, :], in0=od.t[:, qb, :], in1=od.t[:, qb, :], op=ALU.mult), reads=od.b, writes=junk.b)
                                                S.op("dve", lambda e: e.tensor_reduce(out=ss.t[:, qb:qb + 1], in_=junk.t[:, :], axis=mybir.AxisListType.X, op=ALU.add), reads=junk.b, writes=ss.b)

                                    def fin2(c=c, h=h, od=od, ss=ss):
                                        self.act(ss.t[:, :], ss.t[:, :], AF.Sqrt, r=ss.b + self.cst.b, w=ss.b, bias=self.cst.t[:, 1:2], scale=1.0 / 128.0)
                                        S.op("dve", lambda e: e.reciprocal(out=ss.t[:, :], in_=ss.t[:, :]), reads=ss.b, writes=ss.b)
                                        for qb in range(4):
                                            S.op("dve", lambda e: e.scalar_tensor_tensor(out=otm.t[:, 4 * c + qb, h * 128:(h + 1) * 128], in0=od.t[:, qb, :], scalar=ss.t[:, qb:qb + 1],
                                                                                       in1=gsub.t[:, :], op0=ALU.mult, op1=ALU.mult), reads=od.b + ss.b + gsub.b, writes=[otm.b[4 * c + qb]])
                                    unit["fin"] = fin
                                    if cc == 1:
                                        unit["fin2"] = fin2
                                units.append(unit)
                    self.run_units(units)
                S.barrier()
            self.otm_to_oT(otm)
            S.barrier()


_N_CORES = 8


def _run(inputs, n_seq, layers, n_cores, xs, trace=False):
    prog = Prog(n_seq, layers)
    nc = prog.build()
    extra = _host_layout(inputs)
    consts = _consts()
    base = {}
    for k in prog.d:
        if k == "x":
            continue
        if k in consts:
            base[k] = consts[k]
        elif k in extra:
            base[k] = np.ascontiguousarray(extra[k], dtype=np.float32)
        else:
            base[k] = np.ascontiguousarray(np.asarray(inputs[k]), dtype=np.float32)
    in_maps = []
    for c in range(n_cores):
        m = dict(base)
        m["x"] = np.ascontiguousarray(xs[c], dtype=np.float32)
        in_maps.append(m)
    if trace:
        res = run_bass_kernel_spmd(nc, in_maps, core_ids=list(range(n_cores)), trace=True)
        print("EXEC_NS", res.exec_time_ns)
    else:
        res = run_bass_kernel_spmd(nc, in_maps, core_ids=list(range(n_cores)))
    return [r["out"] for r in res.results]


def kernel(**inputs):
    x = np.asarray(inputs["x"], dtype=np.float32)
    B = x.shape[0]
    per = B // _N_CORES
    xs = [x[c * per:(c + 1) * per] for c in range(_N_CORES)]
    outs = _run(inputs, per, [0, 1, 2, 3], _N_CORES, xs)
    return np.concatenate(outs, axis=0).astype(np.float32)
```

```python
import math
from contextlib import ExitStack
import numpy as np
import concourse.bass as bass
import concourse.mybir as mybir
from concourse.bass_utils import run_bass_kernel_spmd

F32 = mybir.dt.float32
BF16 = mybir.dt.bfloat16
AF = mybir.ActivationFunctionType
ALU = mybir.AluOpType

SEQ = 2048
D = 1024
NT = 16
DFF = 4096
DEPTH = 4
ALPHA = (2 * DEPTH) ** 0.25
LN_EPS = 1e-5
NCMP = 127
NEGM = -30000.0
import os
_DBG = os.environ.get('KDBG', '')


class Buf:
    __slots__ = ("w", "r")

    def __init__(self):
        self.w = None
        self.r = {}


class Eng:
    def __init__(self, key, eng, sem):
        self.key = key
        self.eng = eng
        self.sem = sem
        self.count = 0
        self.seen = {}


class Sched:
    def __init__(self, nc, nsem_dma=8):
        self.nc = nc
        self.engs = {}
        self._ctx = []
        for key, eng in (("pe", nc.tensor), ("act", nc.scalar), ("dve", nc.vector), ("pool", nc.gpsimd)):
            cm = nc.semaphore("s_" + key)
            self.engs[key] = Eng(key, eng, cm.__enter__())
            self._ctx.append(cm)
        self.sp = Eng("sp", nc.sync, None)
        self.dsems = []
        for i in range(nsem_dma):
            cm = nc.semaphore(f"d_sp{i}")
            self.dsems.append(cm.__enter__())
            self._ctx.append(cm)
        self.dcnt = [0] * nsem_dma
        self.dn = 0

    def _wait(self, issuer, dep):
        kind, key, val = dep
        k = (kind, id(key) if kind == "d" else key)
        if issuer.seen.get(k, 0) >= val:
            return
        sem = self.engs[key].sem if kind == "e" else key
        issuer.eng.wait_ge(sem, val)
        issuer.seen[k] = val

    @staticmethod
    def _deps(reads, writes):
        deps = []
        for b in reads:
            if b.w is not None:
                deps.append(b.w)
        for b in writes:
            if b.w is not None:
                deps.append(b.w)
            deps.extend(b.r.values())
        return deps

    @staticmethod
    def _mark(reads, writes, tok):
        k = (tok[0], id(tok[1]) if tok[0] == "d" else tok[1])
        for b in reads:
            b.r[k] = tok
        for b in writes:
            b.w = tok
            b.r = {}

    def op(self, ek, fn, reads=(), writes=()):
        e = self.engs[ek]
        for d in self._deps(reads, writes):
            if ek == "pe" and d[0] == "e" and d[1] == "pe":
                continue
            self._wait(e, d)
        ins = fn(e.eng)
        e.count += 1
        ins.then_inc(e.sem, 1)
        self._mark(reads, writes, ("e", ek, e.count))
        return ins

    def dma(self, out, in_, reads=(), writes=(), **kw):
        i = self.dn % len(self.dsems)
        sem = self.dsems[i]
        if self.dcnt[i] > 0:
            self._wait(self.sp, ("d", sem, self.dcnt[i]))
        for d in self._deps(reads, writes):
            self._wait(self.sp, d)
        ins = self.nc.sync.dma_start(out=out, in_=in_, **kw)
        self.dcnt[i] += 16
        ins.then_inc(sem, 16)
        self.dn += 1
        self._mark(reads, writes, ("d", sem, self.dcnt[i]))

    def barrier(self):
        issuers = list(self.engs.values()) + [self.sp]
        for iss in issuers:
            for e2 in self.engs.values():
                if e2.count and e2 is not iss:
                    self._wait(iss, ("e", e2.key, e2.count))
            for i, sem in enumerate(self.dsems):
                if self.dcnt[i]:
                    self._wait(iss, ("d", sem, self.dcnt[i]))

    def close(self):
        for cm in reversed(self._ctx):
            cm.__exit__(None, None, None)


class TT:
    def __init__(self, t, nb=1):
        self.t = t
        self.b = [Buf() for _ in range(nb)]

    def __getitem__(self, k):
        return self.t[k]


def _consts():
    c = {}
    c["c_ident"] = np.eye(128, dtype=np.float32)
    half = 8
    inv = (np.float32(500000.0) ** (-np.arange(half, dtype=np.float32) / np.float32(half))).astype(np.float32)

    def tables(pos):
        pos = pos.astype(np.float32)
        ang = pos[None, :] * inv[:, None]
        C = np.ones((128, pos.shape[0]), np.float32)
        Sn = np.zeros((128, pos.shape[0]), np.float32)
        for base in (0, 64):
            C[base:base + 8] = np.cos(ang)
            C[base + 8:base + 16] = np.cos(ang)
            Sn[base:base + 8] = -np.sin(ang)
            Sn[base + 8:base + 16] = np.sin(ang)
        return C, Sn

    c["c_ropec"], c["c_ropes"] = tables(np.arange(SEQ))
    cc, sc = tables(np.arange(NCMP) * 16 + 31)
    c["c_ropecc"] = np.concatenate([cc, np.ones((128, 1), np.float32)], 1)
    c["c_ropesc"] = np.concatenate([sc, np.zeros((128, 1), np.float32)], 1)
    n = np.arange(128)
    t = np.arange(SEQ)
    valid = ((16 * n[:, None] + 31) <= t[None, :]) & (n[:, None] < NCMP)
    c["c_validc"] = valid.astype(np.float32)
    cs = np.arange(NCMP) * 16
    ce = cs + 31
    ss = np.arange(32) * 64
    ov = ((cs[:, None] <= ss[None, :] + 63) & (ce[:, None] >= ss[None, :])).astype(np.float32)
    ovl = np.zeros((128, 33), np.float32)
    ovl[:NCMP, :32] = ov
    ovl[:NCMP, 32] = 1.0
    c["c_ovl1"] = ovl
    cur = t // 64
    jb = np.arange(32)
    forced = (jb[None, :] == 0) | (jb[None, :] == cur[:, None]) | (jb[None, :] == cur[:, None] - 1)
    future = jb[None, :] > cur[:, None]
    keep = (~forced & ~future).astype(np.float32)
    addc = np.where(forced, 1e4, np.where(future, -1e4, 0.0)).astype(np.float32)
    c["c_keep"] = keep.reshape(NT, 128, 32).transpose(1, 0, 2).reshape(128, NT * 32).copy()
    c["c_addc"] = addc.reshape(NT, 128, 32).transpose(1, 0, 2).reshape(128, NT * 32).copy()
    ee = np.zeros((128, NT, 128), np.float32)
    for kt in range(NT):
        for k in range(128):
            ee[2 * kt + k // 64, kt, k] = 1.0
    c["c_eexp"] = ee.reshape(128, NT * 128)
    es = np.zeros((128, 16, 128), np.float32)
    for h in range(16):
        for r0 in (0, 16, 32, 64, 80, 96):
            es[r0 + h, h, :] = 1.0
    c["c_esel"] = es.reshape(128, 16 * 128)
    return c


_EVEN_OFF = dict(qa=0, ka=512, va=1024, qn=1536, kc=2048, vc=2176, ksl=2304, vsl=2432, kw=2560, vw=2688, g=2816)


def _rope_tiles_even():
    tiles = []
    for h in range(4):
        tiles.append(list(range(_EVEN_OFF["qa"] + 128 * h, _EVEN_OFF["qa"] + 128 * h + 128)))
    for h in range(4):
        tiles.append(list(range(_EVEN_OFF["ka"] + 128 * h, _EVEN_OFF["ka"] + 128 * h + 128)))
    for i in range(4):
        a = _EVEN_OFF["qn"] + 64 * i
        b = _EVEN_OFF["qn"] + 64 * (4 + i)
        tiles.append(list(range(a, a + 64)) + list(range(b, b + 64)))
    tiles.append(list(range(_EVEN_OFF["ksl"], _EVEN_OFF["ksl"] + 128)))
    tiles.append(list(range(_EVEN_OFF["kw"], _EVEN_OFF["kw"] + 128)))
    return tiles


def _host_layout(inp):
    w = inp["w_in_even"]
    tiles = _rope_tiles_even()
    sw = np.zeros((w.shape[0], D, len(tiles) * 128), np.float32)
    main = np.zeros((w.shape[0], D, len(tiles) * 128), np.float32)
    for ti, cols in enumerate(tiles):
        cols = np.array(cols)
        main[:, :, ti * 128:(ti + 1) * 128] = w[:, :, cols]
        for hh in (0, 64):
            sw[:, :, ti * 128 + hh:ti * 128 + hh + 8] = w[:, :, cols[hh + 8:hh + 16]]
            sw[:, :, ti * 128 + hh + 8:ti * 128 + hh + 16] = w[:, :, cols[hh:hh + 8]]
    w2 = inp["nsa_cmp_w2"][:, 0]
    w2sw = np.zeros_like(w2)
    w2sw[:, :, 0:8] = w2[:, :, 8:16]
    w2sw[:, :, 8:16] = w2[:, :, 0:8]
    return dict(w_even_rope=main, w_even_sw=sw, w2_sw=w2sw)


class Prog:
    def __init__(self, n_seq, layers, dbg=False):
        self.n_seq = n_seq
        self.layers = layers
        nc = bass.Bass("TRN2", target_bir_lowering=False)
        self.nc = nc
        self.S = Sched(nc)
        self.d = {}
        self.ucount = 0

    def din(self, name, shape):
        self.d[name] = self.nc.dram_tensor(name, list(shape), F32, kind="ExternalInput").ap()

    def mm(self, out, lhsT, rhs, start, stop, r, w):
        self.S.op("pe", lambda e: e.matmul(out, lhsT=lhsT, rhs=rhs, start=start, stop=stop), reads=r, writes=w)

    def tr(self, out, in_, ident, r, w):
        self.S.op("pe", lambda e: e.transpose(out=out, in_=in_, identity=ident), reads=r, writes=w)

    def act(self, out, in_, func, r, w, bias=None, scale=1.0):
        if bias is None:
            self.S.op("act", lambda e: e.activation(out=out, in_=in_, func=func, scale=scale), reads=r, writes=w)
        else:
            self.S.op("act", lambda e: e.activation(out=out, in_=in_, func=func, bias=bias, scale=scale), reads=r, writes=w)

    def alloc(self, st, name, shape, dt, nb=1):
        self.uid = getattr(self, "uid", 0) + 1
        return TT(st.enter_context(self.nc.sbuf_tensor(f"{name}_{self.uid}", list(shape), dt)), nb)

    def palloc(self, st, name, shape, dt, nb=1):
        return TT(st.enter_context(self.nc.psum_tensor(name, list(shape), dt)), nb)

    def load_w(self, dst_ap, dst_bufs, src_ap, K, n, eng="pool"):
        i = self.stg_i % 2
        self.stg_i += 1
        stg = self.stg[i]
        if K is None:
            v = stg.t[:, 0:n]
        else:
            v = stg.t[:, 0:K * n].rearrange("p (k n) -> p k n", k=K)
        self.S.dma(v, src_ap, writes=stg.b)
        if eng == "act":
            self.S.op(eng, lambda e: e.copy(out=dst_ap, in_=v), reads=stg.b, writes=dst_bufs)
        else:
            self.S.op(eng, lambda e: e.tensor_copy(out=dst_ap, in_=v), reads=stg.b, writes=dst_bufs)

    def load_w_part(self, dst_ap, dst_bufs, src_ap, P0, P1, K, n, eng="pool"):
        i = self.stg_i % 2
        self.stg_i += 1
        stg = self.stg[i]
        v = stg.t[P0:P1, 0:K * n].rearrange("p (k n) -> p k n", k=K)
        self.S.dma(v, src_ap, writes=stg.b)
        if eng == "act":
            self.S.op(eng, lambda e: e.copy(out=dst_ap, in_=v), reads=stg.b, writes=dst_bufs)
        else:
            self.S.op(eng, lambda e: e.tensor_copy(out=dst_ap, in_=v), reads=stg.b, writes=dst_bufs)

    def load_const_bf(self, dst, name, P, n):
        for c0 in range(0, n, 4096):
            w = min(4096, n - c0)
            i = self.stg_i % 2
            self.stg_i += 1
            stg = self.stg[i]
            self.S.dma(stg.t[0:P, 0:w], self.d[name][0:P, c0:c0 + w], writes=stg.b)
            self.S.op("pool", lambda e: e.tensor_copy(out=dst.t[0:P, c0:c0 + w], in_=stg.t[0:P, 0:w]), reads=stg.b, writes=dst.b)

    def build(self):
        nc, S = self.nc, self.S
        ns = self.n_seq
        self.din("x", [ns, SEQ, D])
        self.din("ln_gain", [DEPTH, 2, D])
        self.din("ln_bias", [DEPTH, 2, D])
        self.din("mlp_w_up", [DEPTH, D, DFF])
        self.din("mlp_w_down", [DEPTH, DFF, D])
        self.din("w_in_even", [2, D, 2840])
        self.din("w_even_rope", [2, D, 14 * 128])
        self.din("w_even_sw", [2, D, 14 * 128])
        self.din("w_out_even", [2, D, D])
        self.din("diff_lambda", [2, 4, 64])
        self.din("diff_subln", [2, 128])
        self.din("nsa_pe", [2, 2, 32, 64])
        self.din("nsa_cmp_w1", [2, 2, 2048, 256])
        self.din("nsa_cmp_b1", [2, 2, 256])
        self.din("nsa_cmp_w2", [2, 2, 256, 64])
        self.din("w2_sw", [2, 256, 64])
        self.din("w_in_odd", [2, D, 3088])
        self.din("fox_f_bias", [2, 16])
        self.din("w_out_odd", [2, D, D])
        for k, v in _consts().items():
            self.din(k, v.shape)
        self.out = nc.dram_tensor("out", [ns, SEQ, D], F32, kind="ExternalOutput").ap()
        self.xscr = nc.dram_tensor("xscr", [SEQ, D], F32, kind="Internal").ap()

        with ExitStack() as st:
            self.xT = self.alloc(st, "xT", [128, 8, SEQ], BF16, nb=NT)
            self.stg = [self.alloc(st, f"stg{i}", [128, 4096], F32) for i in range(2)]
            self.stg_i = 0
            self.identb = self.alloc(st, "identb", [128, 128], BF16)
            self.identf = self.alloc(st, "identf", [128, 128], F32)
            self.onesf = self.alloc(st, "onesf", [128, 512], F32)
            self.onesb = self.alloc(st, "onesb", [128, 128], BF16)
            self.cst = self.alloc(st, "cst", [128, 8], F32)
            self.ps = [self.palloc(st, f"ps{i}", [128, 512], F32) for i in range(7)]
            self.psb = self.palloc(st, "psb", [128, 1024], BF16)
            S.dma(self.identf.t[:], self.d["c_ident"][:, :], writes=self.identf.b)
            S.op("pool", lambda e: e.tensor_copy(out=self.identb.t[:], in_=self.identf.t[:]), reads=self.identf.b, writes=self.identb.b)
            S.op("pool", lambda e: e.memset(self.onesf.t[:], 1.0), writes=self.onesf.b)
            S.op("pool", lambda e: e.memset(self.onesb.t[:], 1.0), writes=self.onesb.b)
            S.op("pool", lambda e: e.memset(self.cst.t[:, 0:1], 1.0), writes=self.cst.b)
            S.op("pool", lambda e: e.memset(self.cst.t[:, 1:2], LN_EPS), writes=self.cst.b)
            S.op("pool", lambda e: e.memset(self.cst.t[:, 2:3], 0.0), writes=self.cst.b)
            S.op("pool", lambda e: e.memset(self.cst.t[:, 3:4], 1e-30), writes=self.cst.b)

            for s in range(ns):
                self.phase_init(s)
                for li, layer in enumerate(self.layers):
                    last = li == len(self.layers) - 1
                    if layer % 2 == 0:
                        self.phase_att_even(layer // 2)
                    else:
                        self.phase_att_odd(layer // 2)
                    self.phase_post(s, layer, last)
            S.barrier()
        S.close()
        return nc

    def emit_xT_tile(self, t, xb):
        S = self.S
        for k in range(8):
            self.tr(self.psb.t[:, k * 128:(k + 1) * 128], xb.t[:, k * 128:(k + 1) * 128], self.identb.t[:],
                    r=xb.b + self.identb.b, w=self.psb.b)
        dst = self.xT.t[:, :, t * 128:(t + 1) * 128]
        src = self.psb.t[:, :].rearrange("p (k n) -> p k n", k=8)
        if t % 2 == 0:
            S.op("dve", lambda e: e.tensor_copy(out=dst, in_=src), reads=self.psb.b, writes=[self.xT.b[t]])
        else:
            S.op("act", lambda e: e.copy(out=dst, in_=src), reads=self.psb.b, writes=[self.xT.b[t]])

    def phase_init(self, s):
        S = self.S
        with ExitStack() as st:
            xin = [self.alloc(st, f"xin{i}", [128, D], F32) for i in range(4)]
            xsc = [self.alloc(st, f"xsc{i}", [128, D], F32) for i in range(4)]
            xb = [self.alloc(st, f"xb{i}", [128, D], BF16) for i in range(4)]
            for t in range(NT):
                a, c, b = xin[t % 4], xsc[t % 4], xb[t % 4]
                S.dma(a.t[:], self.d["x"][s, t * 128:(t + 1) * 128, :], writes=a.b)
                S.op("act", lambda e: e.mul(out=c.t[:], in_=a.t[:], mul=float(ALPHA)), reads=a.b, writes=c.b)
                S.dma(self.xscr[t * 128:(t + 1) * 128, :], c.t[:], reads=c.b)
                S.op("dve", lambda e: e.tensor_copy(out=b.t[:], in_=a.t[:]), reads=a.b, writes=b.b)
                self.emit_xT_tile(t, b)
            S.barrier()

    def layernorm_tile(self, src_ap, src_bufs, gb, bb, tmp, y):
        S = self.S
        st6, mv, rs = tmp
        for i in range(2):
            S.op("dve", lambda e: e.bn_stats(out=st6.t[:, i * 6:(i + 1) * 6], in_=src_ap[:, i * 512:(i + 1) * 512]), reads=src_bufs, writes=st6.b)
        S.op("dve", lambda e: e.bn_aggr(out=mv.t[:, 0:2], in_=st6.t[:, 0:12]), reads=st6.b, writes=mv.b)
        self.act(rs.t[:, 0:1], mv.t[:, 1:2], AF.Sqrt, r=mv.b + self.cst.b, w=rs.b, bias=self.cst.t[:, 1:2], scale=1.0)
        S.op("dve", lambda e: e.reciprocal(out=rs.t[:, 0:1], in_=rs.t[:, 0:1]), reads=rs.b, writes=rs.b)
        S.op("dve", lambda e: e.tensor_scalar(out=y.t[:], in0=src_ap, scalar1=mv.t[:, 0:1], scalar2=rs.t[:, 0:1], op0=ALU.subtract, op1=ALU.mult), reads=src_bufs + mv.b + rs.b, writes=y.b)
        S.op("pool", lambda e: e.tensor_tensor(out=y.t[:], in0=y.t[:], in1=gb.t[:], op=ALU.mult), reads=y.b + gb.b, writes=y.b)
        S.op("dve", lambda e: e.tensor_tensor(out=y.t[:], in0=y.t[:], in1=bb.t[:], op=ALU.add), reads=y.b + bb.b, writes=y.b)

    def ln_staged(self, xres, gb, bb, lnb, post_fn):
        S = self.S
        st6s, mvall, rsall = lnb
        for t in range(NT):
            st6 = st6s[t % 4]
            src = xres.t[:, t, :]
            for i in range(2):
                S.op("dve", lambda e: e.bn_stats(out=st6.t[:, i * 6:(i + 1) * 6], in_=src[:, i * 512:(i + 1) * 512]), reads=[xres.b[t]], writes=st6.b)
            S.op("dve", lambda e: e.bn_aggr(out=mvall.t[:, t, 0:2], in_=st6.t[:, 0:12]), reads=st6.b, writes=mvall.b)
        self.act(rsall.t[:, 0:NT], mvall.t[:, :, 1], AF.Sqrt, r=mvall.b + self.cst.b, w=rsall.b, bias=self.cst.t[:, 1:2], scale=1.0)
        S.op("dve", lambda e: e.reciprocal(out=rsall.t[:, 0:NT], in_=rsall.t[:, 0:NT]), reads=rsall.b, writes=rsall.b)
        SK = 3
        for i in range(NT + SK):
            if i < NT:
                t = i
                src = xres.t[:, t, :]
                S.op("dve", lambda e: e.tensor_scalar(out=src, in0=src, scalar1=mvall.t[:, t, 0:1], scalar2=rsall.t[:, t:t + 1], op0=ALU.subtract, op1=ALU.mult),
                     reads=[xres.b[t]] + mvall.b + rsall.b, writes=[xres.b[t]])
                S.op("pool", lambda e: e.tensor_tensor(out=src, in0=src, in1=gb.t[:], op=ALU.mult), reads=[xres.b[t]] + gb.b, writes=[xres.b[t]])
            if i >= SK:
                t = i - SK
                src = xres.t[:, t, :]
                S.op("dve", lambda e: e.tensor_tensor(out=src, in0=src, in1=bb.t[:], op=ALU.add), reads=[xres.b[t]] + bb.b, writes=[xres.b[t]])
                post_fn(t)

    def phase_post(self, s, layer, last):
        S = self.S
        li = layer // 2
        wout = self.d["w_out_even" if layer % 2 == 0 else "w_out_odd"][li]
        with ExitStack() as st:
            xres = self.alloc(st, "xres", [128, NT, D], F32, nb=NT)
            gb = [self.alloc(st, f"gb{i}", [128, D], F32) for i in range(2)]
            bb = [self.alloc(st, f"bb{i}", [128, D], F32) for i in range(2)]
            lnb = ([self.alloc(st, f"st6{i}", [128, 12], F32) for i in range(4)], self.alloc(st, "mvall", [128, NT, 2], F32), self.alloc(st, "rsall", [128, NT], F32))
            xbt = [self.alloc(st, f"xbt{i}", [128, D], BF16) for i in range(4)]
            wup = [self.alloc(st, f"wup{i}", [128, 8, 512], BF16) for i in range(2)]
            wdn = [self.alloc(st, f"wdn{i}", [128, 4, D], BF16) for i in range(2)]
            wup_d = self.d["mlp_w_up"][layer].rearrange("(k p) c -> p k c", p=128)
            wdn_d = self.d["mlp_w_down"][layer].rearrange("(k p) c -> p k c", p=128)

            def load_slice(sl):
                wu, wd = wup[sl % 2], wdn[sl % 2]
                self.load_w(wu.t[:, :, :], wu.b, wup_d[:, :, sl * 512:(sl + 1) * 512], 8, 512)
                self.load_w(wd.t[:, :, :], wd.b, wdn_d[:, sl * 4:(sl + 1) * 4, :], 4, D)
            with ExitStack() as st2:
                wo = self.alloc(st2, "wo", [128, 8, D], BF16, nb=2)
                for hf in range(2):
                    self.load_w(wo.t[:, :, hf * 512:(hf + 1) * 512], [wo.b[hf]],
                                wout.rearrange("(k p) c -> p k c", p=128)[:, :, hf * 512:(hf + 1) * 512], 8, 512, eng=("act" if hf == 0 else "dve"))
                for t in range(NT):
                    S.dma(xres.t[:, t, :], self.xscr[t * 128:(t + 1) * 128, :], writes=[xres.b[t]])
                for i in range(2):
                    S.dma(gb[i].t[:], self.d["ln_gain"][layer, i:i + 1, :].partition_broadcast(128), writes=gb[i].b)
                    S.dma(bb[i].t[:], self.d["ln_bias"][layer, i:i + 1, :].partition_broadcast(128), writes=bb[i].b)
                load_slice(0)
                n = 0
                for t in range(NT):
                    for hf in range(2):
                        pb = self.ps[3 + n % 4]
                        n += 1
                        for k in range(8):
                            self.mm(pb.t[:, :], self.xT.t[:, k, t * 128:(t + 1) * 128], wo.t[:, k, hf * 512:(hf + 1) * 512],
                                    k == 0, k == 7, r=[self.xT.b[t], wo.b[hf]], w=pb.b)
                        dst = xres.t[:, t, hf * 512:(hf + 1) * 512]
                        S.op("dve", lambda e: e.tensor_tensor(out=dst, in0=pb.t[:, :], in1=dst, op=ALU.add), reads=pb.b + [xres.b[t]], writes=[xres.b[t]])
                S.barrier()
            def post1(t):
                xb_ = xbt[t % 4]
                S.op("act", lambda e: e.copy(out=xb_.t[:], in_=xres.t[:, t, :]), reads=[xres.b[t]], writes=xb_.b)
                self.emit_xT_tile(t, xb_)
                S.op("act", lambda e: e.mul(out=xres.t[:, t, :], in_=xres.t[:, t, :], mul=float(ALPHA)), reads=[xres.b[t]], writes=[xres.b[t]])
            self.ln_staged(xres, gb[0], bb[0], lnb, post1)
            with ExitStack() as st2:
                hT = [self.alloc(st2, f"hT{i}", [128, 4, 512], BF16) for i in range(2)]
                n = 0
                hn = 0
                for sl in range(8):
                    wu, wd = wup[sl % 2], wdn[sl % 2]
                    if sl + 1 < 8:
                        load_slice(sl + 1)
                    for c in range(4):
                        h = hT[hn % 2]
                        hn += 1
                        for f in range(4):
                            pb = self.ps[n % 7]
                            n += 1
                            for k in range(8):
                                self.mm(pb.t[:, :], wu.t[:, k, f * 128:(f + 1) * 128], self.xT.t[:, k, c * 512:(c + 1) * 512],
                                        k == 0, k == 7, r=wu.b + self.xT.b[4 * c:4 * c + 4], w=pb.b)
                            self.act(h.t[:, f, :], pb.t[:, :], AF.Relu, r=pb.b, w=h.b)
                            self.act(h.t[:, f, :], h.t[:, f, :], AF.Square, r=h.b, w=h.b)
                        for tt in range(4):
                            t = 4 * c + tt
                            for hf in range(2):
                                pb = self.ps[n % 7]
                                n += 1
                                for f in range(4):
                                    self.mm(pb.t[:, :], h.t[:, f, tt * 128:(tt + 1) * 128], wd.t[:, f, hf * 512:(hf + 1) * 512],
                                            f == 0, f == 3, r=h.b + wd.b, w=pb.b)
                                dst = xres.t[:, t, hf * 512:(hf + 1) * 512]
                                S.op("dve", lambda e: e.tensor_tensor(out=dst, in0=pb.t[:, :], in1=dst, op=ALU.add), reads=pb.b + [xres.b[t]], writes=[xres.b[t]])
                pass
            def post2(t):
                if last:
                    S.dma(self.out[s, t * 128:(t + 1) * 128, :], xres.t[:, t, :], reads=[xres.b[t]])
                else:
                    xb_ = xbt[t % 4]
                    S.op("act", lambda e: e.copy(out=xb_.t[:], in_=xres.t[:, t, :]), reads=[xres.b[t]], writes=xb_.b)
                    self.emit_xT_tile(t, xb_)
                    S.op("act", lambda e: e.mul(out=xres.t[:, t, :], in_=xres.t[:, t, :], mul=float(ALPHA)), reads=[xres.b[t]], writes=[xres.b[t]])
                    S.dma(self.xscr[t * 128:(t + 1) * 128, :], xres.t[:, t, :], reads=[xres.b[t]])
            self.ln_staged(xres, gb[1], bb[1], lnb, post2)
            S.barrier()

    def run_units(self, units, look=2):
        n = len(units)
        pending = []
        for i in range(n + look):
            if i < n:
                units[i]["qk"]()
                units[i]["exp"]()
            for (due, f2) in [p_ for p_ in pending if p_[0] <= i]:
                f2()
            pending = [p_ for p_ in pending if p_[0] > i]
            if i >= look:
                units[i - look]["pv"]()
                f = units[i - look].get("fin")
                if f:
                    f()
                f2 = units[i - look].get("fin2")
                if f2:
                    pending.append((i + 3, f2))
        for (due, f2) in pending:
            f2()

    def otm_to_oT(self, otm):
        for t in range(NT):
            tmp = TT(otm.t[:, t, :])
            tmp.b = [otm.b[t]]
            self.emit_xT_tile(t, tmp)

    def causal_fix(self, pT, col0):
        blk = pT.t[:, col0:col0 + 128]
        self.S.op("pool", lambda e: e.affine_select(out=blk, in_=blk, pattern=[[1, 128]], compare_op=ALU.is_ge, fill=0.0,
                                                      base=0, channel_multiplier=-1), reads=pT.b, writes=pT.b)

    def phase_att_odd(self, li):
        S = self.S
        W = self.d["w_in_odd"][li].rearrange("(k p) c -> p k c", p=128)
        with ExitStack() as st:
            otm = self.alloc(st, "otm", [128, NT, D], BF16, nb=NT)
            haug = self.alloc(st, "haug", [128, SEQ], BF16)
            esel = self.alloc(st, "esel", [128, 16 * 128], BF16)
            cneg_tm = self.alloc(st, "cneg_tm", [128, NT * 16], F32)
            self.load_const_bf(esel, "c_esel", 128, 16 * 128)
            with ExitStack() as st2:
                wf = self.alloc(st2, "wf", [128, 8, 16], BF16)
                fb = self.alloc(st2, "fb", [16, 1], F32)
                cneg = self.alloc(st2, "cneg", [16, SEQ], F32)
                q3 = self.alloc(st2, "q3", [16, SEQ], F32)
                tf = self.alloc(st2, "tf", [16, SEQ], F32)
                tb = self.alloc(st2, "tb", [16, SEQ], BF16)
                self.load_w(wf.t[:, :, :], wf.b, W[:, :, 3072:3088], 8, 16)
                S.dma(fb.t[:, :], self.d["fox_f_bias"][li:li + 1, :].rearrange("o h -> h o"), writes=fb.b, allow_slow_non_contiguous=True)
                S.op("dve", lambda e: e.tensor_scalar_mul(out=fb.t[:, :], in0=fb.t[:, :], scalar1=-1.0), reads=fb.b, writes=fb.b)
                S.op("pool", lambda e: e.memset(haug.t[:, :], 0.0), writes=haug.b)
                for c in range(4):
                    pb = self.ps[3 + c % 4]
                    for k in range(8):
                        self.mm(pb.t[0:16, :], wf.t[:, k, :], self.xT.t[:, k, c * 512:(c + 1) * 512], k == 0, k == 7,
                                r=wf.b + self.xT.b[4 * c:4 * c + 4], w=pb.b)
                    self.act(tf.t[:, c * 512:(c + 1) * 512], pb.t[0:16, :], AF.Exp, r=pb.b + fb.b, w=tf.b, bias=fb.t[:, 0:1], scale=-1.0)
                    self.act(tf.t[:, c * 512:(c + 1) * 512], tf.t[:, c * 512:(c + 1) * 512], AF.Ln, r=tf.b + self.cst.b, w=tf.b, bias=self.cst.t[0:16, 0:1], scale=1.0)
                    init = 0.0 if c == 0 else cneg.t[:, c * 512 - 1:c * 512]
                    S.op("dve", lambda e: e.tensor_tensor_scan(out=cneg.t[:, c * 512:(c + 1) * 512], data0=self.onesf.t[0:16, :], data1=tf.t[:, c * 512:(c + 1) * 512],
                                                                 initial=init, op0=ALU.mult, op1=ALU.add), reads=tf.b + self.onesf.b + cneg.b, writes=cneg.b)
                pb = self.ps[3]
                for t in range(NT):
                    self.tr(pb.t[:, t * 16:(t + 1) * 16], cneg.t[:, t * 128:(t + 1) * 128], self.identf.t[0:16, 0:16], r=cneg.b + self.identf.b, w=pb.b)
                S.op("dve", lambda e: e.tensor_copy(out=cneg_tm.t[:, :], in_=pb.t[:, 0:256]), reads=pb.b, writes=cneg_tm.b)
                S.op("dve", lambda e: e.tensor_scalar_mul(out=q3.t[:, :], in0=cneg.t[:, :], scalar1=-8.0), reads=cneg.b, writes=q3.b)
                S.op("act", lambda e: e.copy(out=haug.t[0:16, :], in_=q3.t[:, :]), reads=q3.b, writes=haug.b)
                S.op("dve", lambda e: e.tensor_copy(out=tf.t[:, :], in_=haug.t[0:16, :]), reads=haug.b, writes=tf.b)
                S.op("dve", lambda e: e.tensor_tensor(out=q3.t[:, :], in0=q3.t[:, :], in1=tf.t[:, :], op=ALU.subtract), reads=q3.b + tf.b, writes=q3.b)
                S.op("act", lambda e: e.copy(out=tb.t[:, :], in_=q3.t[:, :]), reads=q3.b, writes=tb.b)
                S.op("act", lambda e: e.copy(out=haug.t[32:48, :], in_=tb.t[:, :]), reads=tb.b, writes=haug.b)
                S.op("dve", lambda e: e.tensor_copy(out=tf.t[:, :], in_=tb.t[:, :]), reads=tb.b, writes=tf.b)
                S.op("dve", lambda e: e.tensor_tensor(out=q3.t[:, :], in0=q3.t[:, :], in1=tf.t[:, :], op=ALU.subtract), reads=q3.b + tf.b, writes=q3.b)
                S.op("act", lambda e: e.copy(out=tb.t[:, :], in_=q3.t[:, :]), reads=q3.b, writes=tb.b)
                S.op("act", lambda e: e.copy(out=haug.t[64:80, :], in_=tb.t[:, :]), reads=tb.b, writes=haug.b)
                S.barrier()
            with ExitStack() as st2:
                wq = [self.alloc(st2, f"wq{i}", [128, 8, 128], BF16) for i in range(2)]
                wk = [self.alloc(st2, f"wk{i}", [128, 8, 128], BF16) for i in range(2)]
                wv = [self.alloc(st2, f"wv{i}", [128, 8, 128], BF16) for i in range(2)]
                qT = [[self.alloc(st2, f"qT{i}{hh}", [128, SEQ], BF16, nb=5) for hh in range(2)] for i in range(2)]
                kT = [[self.alloc(st2, f"kT{i}{hh}", [128, SEQ], BF16, nb=5) for hh in range(2)] for i in range(2)]
                for i in range(2):
                    for hh in range(2):
                        S.op("pool", lambda e: e.memset(qT[i][hh].t[:, :], 0.0), writes=qT[i][hh].b)
                        ab = 64 if hh == 0 else 0
                        for r, src0 in enumerate((0, 32, 64)):
                            S.dma(qT[i][hh].t[ab + 16 * r:ab + 16 * r + 16, :], haug.t[src0:src0 + 16, :], reads=haug.b, writes=[qT[i][hh].b[4]])
                va = [self.alloc(st2, f"va{i}", [128, NT, 2, 66], BF16) for i in range(2)]
                pT = [self.alloc(st2, f"pT{i}", [128, 512], BF16) for i in range(3)]
                rinv = [self.alloc(st2, f"rinv{i}", [128, 4], F32) for i in range(2)]
                for i in range(2):
                    S.op("pool", lambda e: e.memset(va[i].t[:, :, :, 64:66], 1.0), writes=va[i].b)
                pn = 0
                fin_n = 0
                def load_pair(hp):
                    j = hp % 2
                    self.load_w(wq[j].t[:, :, :], wq[j].b, W[:, :, hp * 128:(hp + 1) * 128], 8, 128)
                    self.load_w(wk[j].t[:, :, :], wk[j].b, W[:, :, 1024 + hp * 128:1024 + (hp + 1) * 128], 8, 128)
                    self.load_w(wv[j].t[:, :, :], wv[j].b, W[:, :, 2048 + hp * 128:2048 + (hp + 1) * 128], 8, 128)
                    for hh in range(2):
                        h = hp * 2 + hh
                        sb = 64 if hh == 0 else 0
                        S.op("pool", lambda e: e.tensor_copy(out=kT[j][hh].t[sb:sb + 64, :].rearrange("p (t k) -> p t k", t=NT),
                                                               in_=esel.t[sb:sb + 64, h * 128:(h + 1) * 128].unsqueeze(1).broadcast_to([64, NT, 128])),
                             reads=esel.b, writes=[kT[j][hh].b[4]])
                load_pair(0)
                for hp in range(8):
                    j = hp % 2
                    for (wt, dstT) in ((wq[j], qT[j]), (wk[j], kT[j])):
                        for c in range(4):
                            pb = self.ps[3 + pn % 4]
                            pn += 1
                            for k in range(8):
                                self.mm(pb.t[:, :], wt.t[:, k, :], self.xT.t[:, k, c * 512:(c + 1) * 512], k == 0, k == 7,
                                        r=wt.b + self.xT.b[4 * c:4 * c + 4], w=pb.b)
                            S.op("dve", lambda e: e.tensor_copy(out=dstT[0].t[0:64, c * 512:(c + 1) * 512], in_=pb.t[0:64, :]), reads=pb.b, writes=[dstT[0].b[c]])
                            S.op("dve", lambda e: e.tensor_copy(out=dstT[1].t[64:128, c * 512:(c + 1) * 512], in_=pb.t[64:128, :]), reads=pb.b, writes=[dstT[1].b[c]])
                    for g4 in range(4):
                        pb = self.ps[3 + pn % 4]
                        pn += 1
                        for tt in range(4):
                            t = g4 * 4 + tt
                            for k in range(8):
                                self.mm(pb.t[:, tt * 128:(tt + 1) * 128], self.xT.t[:, k, t * 128:(t + 1) * 128], wv[j].t[:, k, :], k == 0, k == 7,
                                        r=wv[j].b + [self.xT.b[t]], w=pb.b)
                        S.op("dve", lambda e: e.tensor_copy(out=va[j].t[:, g4 * 4:(g4 + 1) * 4, :, 0:64],
                                                              in_=pb.t[:, :].rearrange("p (t h d) -> p t h d", t=4, h=2)), reads=pb.b, writes=va[j].b)
                    if hp + 1 < 8:
                        load_pair(hp + 1)
                    units = []
                    for hh in range(2):
                        h = hp * 2 + hh
                        base = hh * 64
                        for c in range(4):
                            oacc = self.ps[3 + fin_n % 4]
                            ri = rinv[fin_n % 2]
                            fin_n += 1
                            nk = 4 * c + 4
                            for kt in range(nk):
                                u = self.ucount
                                self.ucount += 1
                                sc = self.ps[u % 3]
                                p = pT[u % 3]
                                col0 = 0 if kt < 4 * c else (kt - 4 * c) * 128
                                diag = kt >= 4 * c

                                def qk(sc=sc, kt=kt, c=c, col0=col0, hh=hh, j=j):
                                    self.mm(sc.t[:, col0:512], kT[j][hh].t[:, kt * 128:(kt + 1) * 128],
                                            qT[j][hh].t[:, c * 512 + col0:(c + 1) * 512], True, True,
                                            r=[kT[j][hh].b[kt // 4], kT[j][hh].b[4], qT[j][hh].b[c], qT[j][hh].b[4]], w=sc.b)

                                def ex(sc=sc, p=p, kt=kt, col0=col0, h=h, diag=diag):
                                    self.act(p.t[:, col0:512], sc.t[:, col0:512], AF.Exp, r=sc.b + cneg_tm.b, w=p.b,
                                             bias=cneg_tm.t[:, kt * 16 + h:kt * 16 + h + 1], scale=0.125)
                                    if diag:
                                        self.causal_fix(p, col0)

                                def pv(p=p, kt=kt, c=c, col0=col0, hh=hh, oacc=oacc, j=j):
                                    ov = oacc.t[:, :].rearrange("p (q e) -> p q e", q=4)
                                    for qb in range(col0 // 128, 4):
                                        self.mm(ov[:, qb, 0:65], p.t[:, qb * 128:(qb + 1) * 128], va[j].t[:, kt, hh, 0:65],
                                                kt == 0 and qb == 0, kt == 4 * c + 3 and qb == 3, r=p.b + va[j].b, w=oacc.b)

                                unit = dict(qk=qk, exp=ex, pv=pv)
                                if kt == nk - 1:
                                    def fin(oacc=oacc, ri=ri, c=c, h=h):
                                        ov = oacc.t[:, :].rearrange("p (q e) -> p q e", q=4)
                                        S.op("dve", lambda e: e.reciprocal(out=ri.t[:, :], in_=ov[:, :, 64]), reads=oacc.b, writes=ri.b)
                                        for qb in range(4):
                                            S.op("dve", lambda e: e.tensor_scalar_mul(out=otm.t[:, 4 * c + qb, h * 64:(h + 1) * 64], in0=ov[:, qb, 0:64],
                                                                                    scalar1=ri.t[:, qb:qb + 1]), reads=oacc.b + ri.b, writes=[otm.b[4 * c + qb]])
                                    unit["fin"] = fin
                                units.append(unit)
                    self.run_units(units)
                pass
            self.otm_to_oT(otm)
            S.barrier()

    def proj_fm(self, wts, dst, evac):
        for c in range(4):
            pbs = []
            for wt in wts:
                pb = self.ps[3 + self.pn % 4]
                self.pn += 1
                for k in range(8):
                    self.mm(pb.t[:, :], wt.t[:, k, :], self.xT.t[:, k, c * 512:(c + 1) * 512], k == 0, k == 7,
                            r=wt.b + self.xT.b[4 * c:4 * c + 4], w=pb.b)
                pbs.append(pb)
            evac(c, pbs)

    def rope_evac(self, parts, ropec, ropes, tmps):
        S = self.S

        def ev(c, pbs):
            t1, t2 = tmps[self.tn % 2]
            self.tn += 1
            sl = slice(c * 512, (c + 1) * 512)
            S.op("dve", lambda e: e.tensor_tensor(out=t1.t[:, :], in0=pbs[0].t[:, :], in1=ropec.t[:, sl], op=ALU.mult), reads=pbs[0].b + ropec.b, writes=t1.b)
            S.op("dve", lambda e: e.tensor_tensor(out=t2.t[:, :], in0=pbs[1].t[:, :], in1=ropes.t[:, sl], op=ALU.mult), reads=pbs[1].b + ropes.b, writes=t2.b)
            for pi, (dst, p0, p1) in enumerate(parts):
                eng = "pool" if (pi + c) % 2 == 0 else "dve"
                S.op(eng, lambda e: e.tensor_tensor(out=dst.t[p0:p1, sl], in0=t1.t[p0:p1, :], in1=t2.t[p0:p1, :], op=ALU.add), reads=t1.b + t2.b, writes=[dst.b[c]])
        return ev

    def plain_evac(self, dst):
        def ev(c, pbs):
            self.S.op("dve", lambda e: e.tensor_copy(out=dst.t[:, c * 512:(c + 1) * 512], in_=pbs[0].t[:, :]), reads=pbs[0].b, writes=[dst.b[c]])
        return ev

    def proj_tm(self, wv, ncols, evac):
        per = max(1, 512 // ncols)
        per = min(per, 4)
        for t0 in range(0, NT, per):
            pb = self.ps[3 + self.pn % 4]
            self.pn += 1
            for tt in range(per):
                t = t0 + tt
                for k in range(8):
                    self.mm(pb.t[:, tt * ncols:(tt + 1) * ncols], self.xT.t[:, k, t * 128:(t + 1) * 128], wv.t[:, k, 0:ncols], k == 0, k == 7,
                            r=wv.b + [self.xT.b[t]], w=pb.b)
            evac(t0, per, pb)

    def phase_att_even(self, li):
        S = self.S
        layer = 2 * li
        Wr = self.d["w_even_rope"][li].rearrange("(k p) c -> p k c", p=128)
        Ws = self.d["w_even_sw"][li].rearrange("(k p) c -> p k c", p=128)
        Wn = self.d["w_in_even"][li].rearrange("(k p) c -> p k c", p=128)
        self.pn = 0
        self.tn = 0
        lam_init = 0.8 - 0.6 * math.exp(-0.3 * layer)
        with ExitStack() as st:
            otm = self.alloc(st, "otm", [128, NT, D], BF16, nb=NT)
            ropec = self.alloc(st, "ropec", [128, SEQ], F32)
            ropes = self.alloc(st, "ropes", [128, SEQ], F32)
            tmps = [(self.alloc(st, f"rt1{i}", [128, 512], F32), self.alloc(st, f"rt2{i}", [128, 512], F32)) for i in range(2)]
            pT = [self.alloc(st, f"pT{i}", [128, 512], BF16) for i in range(3)]
            for c in range(4):
                S.dma(ropec.t[:, c * 512:(c + 1) * 512], self.d["c_ropec"][:, c * 512:(c + 1) * 512], writes=ropec.b)
                S.dma(ropes.t[:, c * 512:(c + 1) * 512], self.d["c_ropes"][:, c * 512:(c + 1) * 512], writes=ropes.b)
            with ExitStack() as st1:
                qn = [self.alloc(st1, f"qn{i}", [128, SEQ], BF16, nb=4) for i in range(8)]
                ksl = [self.alloc(st1, f"ksl{g}", [128, SEQ], BF16, nb=5) for g in range(2)]
                kw = [self.alloc(st1, f"kw{g}", [128, SEQ], BF16, nb=4) for g in range(2)]
                for tq in qn + ksl + kw:
                    S.op("pool", lambda e: e.memset(tq.t[:, :], 0.0), writes=tq.b)
                for g in range(2):
                    m0 = 64 if g == 0 else 0
                    i_ = self.stg_i % 2
                    self.stg_i += 1
                    stg = self.stg[i_]
                    S.dma(stg.t[m0:m0 + 32, 0:NT * 128], self.d["c_eexp"][0:32, :], writes=stg.b)
                    S.op("pool", lambda e: e.tensor_copy(out=ksl[g].t[m0:m0 + 32, :], in_=stg.t[m0:m0 + 32, 0:NT * 128]), reads=stg.b, writes=[ksl[g].b[4]])
                vsl = self.alloc(st1, "vsl", [128, NT, 2, 66], BF16)
                vw = self.alloc(st1, "vw", [128, NT, 2, 66], BF16)
                gates = self.alloc(st1, "gates", [128, NT, 24], F32)
                kcT2 = [self.alloc(st1, f"kcT2{g}", [128, 128], BF16) for g in range(2)]
                RC = self.alloc(st1, "RC", [128, 2, 98], BF16)
                S.op("pool", lambda e: e.memset(vsl.t[:, :, :, 64:66], 1.0), writes=vsl.b)
                S.op("pool", lambda e: e.memset(vw.t[:, :, :, 64:66], 1.0), writes=vw.b)
                S.op("pool", lambda e: e.memset(RC.t[:, :, :], 0.0), writes=RC.b)
                with ExitStack() as st2:
                    wA = [self.alloc(st2, f"wA{i}", [128, 8, 128], BF16) for i in range(2)]
                    wB = [self.alloc(st2, f"wB{i}", [128, 8, 128], BF16) for i in range(2)]
                    cmpT = []
                    for j in range(2):
                        tcm = TT(otm.t[:, 8 + 2 * j:10 + 2 * j, :].rearrange("p a b -> p (a b)"), nb=4)
                        cmpT.append(tcm)
                    w1t = TT(otm.t[:, 0:8, :].rearrange("p a (b c) -> p (a b) c", c=256))
                    hidT = [[self.alloc(st2, f"hidT{j}{g}", [128, 2, 128], BF16) for g in range(2)] for j in range(2)]
                    peT = self.alloc(st2, "peT", [64, 32], BF16)
                    b1t = self.alloc(st2, "b1t", [128, 2], F32)
                    bias2 = self.alloc(st2, "bias2", [128, 2], F32)
                    w2p = self.alloc(st2, "w2p", [128, 2, 192], BF16)
                    w2sp = self.alloc(st2, "w2sp", [128, 2, 192], BF16)
                    w2v = self.alloc(st2, "w2v", [128, 2, 64], BF16)
                    gz = [self.alloc(st2, f"gz{i}", [128, 128], F32) for i in range(3)]
                    ccs = self.alloc(st2, "ccs", [128, 128], F32)
                    scs = self.alloc(st2, "scs", [128, 128], F32)
                    ovl = self.alloc(st2, "ovl", [128, 33], F32)
                    stages = []
                    for ti, parts in [(8, [(qn[0], 0, 64), (qn[4], 64, 128)]), (9, [(qn[1], 0, 64), (qn[5], 64, 128)]),
                                      (10, [(qn[2], 0, 64), (qn[6], 64, 128)]), (11, [(qn[3], 0, 64), (qn[7], 64, 128)]),
                                      (12, [(ksl[0], 0, 64), (ksl[1], 64, 128)]), (13, [(kw[0], 0, 64), (kw[1], 64, 128)])]:
                        def ld(i, ti=ti):
                            a, b = wA[i % 2], wB[i % 2]
                            self.load_w(a.t[:, :, :], a.b, Wr[:, :, ti * 128:(ti + 1) * 128], 8, 128)
                            self.load_w(b.t[:, :, :], b.b, Ws[:, :, ti * 128:(ti + 1) * 128], 8, 128)

                        def cp(i, parts=parts):
                            self.proj_fm([wA[i % 2], wB[i % 2]], None, self.rope_evac(parts, ropec, ropes, tmps))
                        stages.append((ld, cp))
                    for j, off in ((0, _EVEN_OFF["kc"]), (1, _EVEN_OFF["vc"])):
                        def ld(i, off=off):
                            a = wA[i % 2]
                            self.load_w(a.t[:, :, :], a.b, Wn[:, :, off:off + 128], 8, 128)

                        def cp(i, j=j):
                            self.proj_fm([wA[i % 2]], cmpT[j], self.plain_evac(cmpT[j]))
                        stages.append((ld, cp))
                    for off, dstv in ((_EVEN_OFF["vsl"], vsl), (_EVEN_OFF["vw"], vw)):
                        def ld(i, off=off):
                            a = wA[i % 2]
                            self.load_w(a.t[:, :, :], a.b, Wn[:, :, off:off + 128], 8, 128)

                        def cp(i, dstv=dstv):
                            def ev(t0, per, pb, dstv=dstv):
                                S.op("dve", lambda e: e.tensor_copy(out=dstv.t[:, t0:t0 + per, :, 0:64],
                                                                      in_=pb.t[:, :].rearrange("p (t g d) -> p t g d", t=per, g=2)), reads=pb.b, writes=dstv.b)
                            self.proj_tm(wA[i % 2], 128, ev)
                        stages.append((ld, cp))

                    def ldg(i):
                        a = wA[i % 2]
                        self.load_w(a.t[:, :, 0:24], a.b, Wn[:, :, _EVEN_OFF["g"]:_EVEN_OFF["g"] + 24], 8, 24)

                    def cpg(i):
                        def evg(t0, per, pb):
                            self.act(gates.t[:, t0:t0 + per, :], pb.t[:, 0:per * 24].rearrange("p (t g) -> p t g", t=per), AF.Sigmoid, r=pb.b, w=gates.b)
                        self.proj_tm(wA[i % 2], 24, evg)
                    stages.append((ldg, cpg))
                    def load_w1(j):
                        w1d = self.d["nsa_cmp_w1"][li, j].rearrange("(l d) c -> d l c", d=64)
                        for half in range(2):
                            for lh in range(2):
                                self.load_w_part(w1t.t[half * 64:(half + 1) * 64, lh * 16:(lh + 1) * 16, :], w1t.b,
                                                 w1d[:, lh * 16:(lh + 1) * 16, :], half * 64, (half + 1) * 64, 16, 256,
                                                 eng=("dve" if lh == 0 else "act"))
                    load_w1(0)
                    stages[0][0](0)
                    for i, (ld_, cp_) in enumerate(stages):
                        if i + 1 < len(stages):
                            stages[i + 1][0](i + 1)
                        cp_(i)
                    S.dma(ccs.t[:, :], self.d["c_ropecc"][:, :], writes=ccs.b)
                    S.dma(scs.t[:, :], self.d["c_ropesc"][:, :], writes=scs.b)
                    S.dma(ovl.t[:, :], self.d["c_ovl1"][:, :], writes=ovl.b)
                    S.op("pool", lambda e: e.memset(w2p.t[:, :, :], 0.0), writes=w2p.b)
                    S.op("pool", lambda e: e.memset(w2sp.t[:, :, :], 0.0), writes=w2sp.b)
                    self.load_w(w2p.t[:, :, 64:128], w2p.b, self.d["nsa_cmp_w2"][li, 0].rearrange("(m p) c -> p m c", p=128), 2, 64)
                    self.load_w(w2sp.t[:, :, 64:128], w2sp.b, self.d["w2_sw"][li].rearrange("(m p) c -> p m c", p=128), 2, 64)
                    self.load_w(w2v.t[:, :, :], w2v.b, self.d["nsa_cmp_w2"][li, 1].rearrange("(m p) c -> p m c", p=128), 2, 64)
                    for j in range(2):
                        if j == 1:
                            load_w1(1)
                        i = self.stg_i % 2
                        self.stg_i += 1
                        stg = self.stg[i]
                        S.dma(stg.t[0:64, 0:32], self.d["nsa_pe"][li, j].rearrange("l d -> d l"), writes=stg.b, allow_slow_non_contiguous=True)
                        S.op("pool", lambda e: e.tensor_copy(out=peT.t[:, :], in_=stg.t[0:64, 0:32]), reads=stg.b, writes=peT.b)
                        S.dma(b1t.t[:, :], self.d["nsa_cmp_b1"][li, j].rearrange("(m p) -> p m", p=128), writes=b1t.b, allow_slow_non_contiguous=True)
                        pbias = self.ps[3 + self.pn % 4]
                        self.pn += 1
                        for m in range(2):
                            for l in range(32):
                                self.mm(pbias.t[:, m:m + 1], w1t.t[0:64, l, m * 128:(m + 1) * 128], peT.t[0:64, l:l + 1], l == 0, l == 31,
                                        r=w1t.b + peT.b, w=pbias.b)
                        S.op("dve", lambda e: e.tensor_tensor(out=bias2.t[:, :], in0=pbias.t[:, 0:2], in1=b1t.t[:, :], op=ALU.add), reads=pbias.b + b1t.b, writes=bias2.b)
                        for g in range(2):
                            src = cmpT[j].t[:, :].rearrange("p (n s) -> p n s", s=16)
                            for m in range(2):
                                ph = self.ps[3 + self.pn % 4]
                                self.pn += 1
                                for l in range(32):
                                    self.mm(ph.t[:, 0:NCMP], w1t.t[g * 64:(g + 1) * 64, l, m * 128:(m + 1) * 128],
                                            src[g * 64:(g + 1) * 64, l // 16:l // 16 + NCMP, l % 16], l == 0, l == 31,
                                            r=w1t.b + cmpT[j].b, w=ph.b)
                                z, u, sg = gz
                                self.act(z.t[:, 0:NCMP], ph.t[:, 0:NCMP], AF.Identity, r=ph.b + bias2.b, w=z.b, bias=bias2.t[:, m:m + 1], scale=1.0)
                                S.op("pool", lambda e: e.tensor_tensor(out=u.t[:, 0:NCMP], in0=z.t[:, 0:NCMP], in1=z.t[:, 0:NCMP], op=ALU.mult), reads=z.b, writes=u.b)
                                S.op("dve", lambda e: e.tensor_scalar(out=u.t[:, 0:NCMP], in0=u.t[:, 0:NCMP], scalar1=0.044715, scalar2=1.0, op0=ALU.mult, op1=ALU.add), reads=u.b, writes=u.b)
                                S.op("pool", lambda e: e.tensor_tensor(out=u.t[:, 0:NCMP], in0=u.t[:, 0:NCMP], in1=z.t[:, 0:NCMP], op=ALU.mult), reads=u.b + z.b, writes=u.b)
                                self.act(sg.t[:, 0:NCMP], u.t[:, 0:NCMP], AF.Sigmoid, r=u.b, w=sg.b, scale=2.0 * 0.7978845608028654)
                                S.op("dve", lambda e: e.tensor_tensor(out=hidT[j][g].t[:, m, 0:NCMP], in0=z.t[:, 0:NCMP], in1=sg.t[:, 0:NCMP], op=ALU.mult), reads=z.b + sg.b, writes=hidT[j][g].b)
                    pa = self.ps[3 + self.pn % 4]
                    self.pn += 1
                    pb2 = self.ps[3 + self.pn % 4]
                    self.pn += 1
                    for (pp, wt) in ((pa, w2p), (pb2, w2sp)):
                        n = 0
                        for g in range(2):
                            for m in range(2):
                                lhs = wt.t[:, m, 64:192] if g == 0 else wt.t[:, m, 0:128]
                                self.mm(pp.t[:, 0:NCMP], lhs, hidT[0][g].t[:, m, 0:NCMP], n == 0, n == 3, r=wt.b + hidT[0][g].b, w=pp.b)
                                n += 1
                    z, u, sg = gz
                    S.op("dve", lambda e: e.tensor_tensor(out=z.t[:, 0:NCMP], in0=pa.t[:, 0:NCMP], in1=ccs.t[:, 0:NCMP], op=ALU.mult), reads=pa.b + ccs.b, writes=z.b)
                    S.op("dve", lambda e: e.tensor_tensor(out=u.t[:, 0:NCMP], in0=pb2.t[:, 0:NCMP], in1=scs.t[:, 0:NCMP], op=ALU.mult), reads=pb2.b + scs.b, writes=u.b)
                    for g in range(2):
                        S.op("pool", lambda e: e.memset(kcT2[g].t[:, :], 0.0), writes=kcT2[g].b)
                        S.op("pool", lambda e: e.tensor_tensor(out=kcT2[g].t[g * 64:(g + 1) * 64, 0:NCMP], in0=z.t[g * 64:(g + 1) * 64, 0:NCMP],
                                                                 in1=u.t[g * 64:(g + 1) * 64, 0:NCMP], op=ALU.add), reads=z.b + u.b, writes=kcT2[g].b)
                    for g in range(2):
                        pv_ = self.ps[3 + self.pn % 4]
                        self.pn += 1
                        for m in range(2):
                            self.mm(pv_.t[0:NCMP, 0:64], hidT[1][g].t[:, m, 0:NCMP], w2v.t[:, m, :], m == 0, m == 1, r=hidT[1][g].b + w2v.b, w=pv_.b)
                        S.op("dve", lambda e: e.tensor_copy(out=RC.t[0:NCMP, g, 0:64], in_=pv_.t[0:NCMP, 0:64]), reads=pv_.b, writes=RC.b)
                        S.op("pool", lambda e: e.tensor_copy(out=RC.t[:, g, 64:97], in_=ovl.t[:, :]), reads=ovl.b, writes=RC.b)
                    S.barrier()
                with ExitStack() as st2:
                    validc = self.alloc(st2, "validc", [128, SEQ], BF16)
                    keep = self.alloc(st2, "keep", [128, NT * 32], F32)
                    addc = self.alloc(st2, "addc", [128, NT * 32], F32)
                    self.load_const_bf(validc, "c_validc", 128, SEQ)
                    S.op("pool", lambda e: e.tensor_scalar(out=validc.t[:, :], in0=validc.t[:, :], scalar1=-1.0, scalar2=30000.0, op0=ALU.add, op1=ALU.mult), reads=validc.b, writes=validc.b)
                    S.dma(keep.t[:, :], self.d["c_keep"][:, :], writes=keep.b)
                    S.dma(addc.t[:, :], self.d["c_addc"][:, :], writes=addc.b)
                    octmps = [self.alloc(st2, f"octmp{g}", [128, 4, 4, 64], F32) for g in range(2)]
                    imps = [self.alloc(st2, f"imp{g}", [128, 4, 32], F32) for g in range(2)]
                    imp2 = self.alloc(st2, "imp2", [128, 4, 32], F32)
                    mx8 = self.alloc(st2, "mx8", [128, 4, 8], F32)
                    negb = self.alloc(st2, "negb", [128, 4, 32], BF16)
                    negT = [self.alloc(st2, f"negT{g}", [32, 512], BF16) for g in range(2)]
                    osum = [self.alloc(st2, f"osum{i}", [128, 4, 64], F32) for i in range(2)]
                    rcp = [self.alloc(st2, f"rcp{i}", [128, 4], F32) for i in range(3)]
                    coef = [self.alloc(st2, f"coef{i}", [128, 4], F32) for i in range(3)]
                    fn = 0
                    for c in range(0 if "nonsamain" in _DBG else 4):
                        cunits = []
                        for g in range(2):
                            octmp = octmps[g]
                            imp = imps[g]
                            units = cunits
                            for i in range(4):
                                h = g * 4 + i
                                u_ = self.ucount
                                self.ucount += 1
                                sc = self.ps[u_ % 3]
                                p = pT[u_ % 3]
                                acc = self.ps[3 + fn % 4]
                                rc_, cf_ = rcp[fn % 3], coef[fn % 3]
                                fn += 1

                                def qk(sc=sc, h=h, g=g, c=c):
                                    self.mm(sc.t[0:NCMP, :], kcT2[g].t[:, 0:NCMP], qn[h].t[:, c * 512:(c + 1) * 512], True, False,
                                            r=kcT2[g].b + [qn[h].b[c]], w=sc.b)
                                    self.mm(sc.t[0:NCMP, :], self.identb.t[0:NCMP, 0:NCMP], validc.t[0:NCMP, c * 512:(c + 1) * 512], False, True,
                                            r=self.identb.b + validc.b, w=sc.b)

                                def ex(sc=sc, p=p, c=c):
                                    self.act(p.t[0:NCMP, :], sc.t[0:NCMP, :], AF.Exp, r=sc.b, w=p.b, scale=0.125)

                                def pv(p=p, acc=acc, g=g):
                                    av = acc.t[:, :].rearrange("p (q e) -> p q e", q=4)
                                    for qb in range(4):
                                        self.mm(av[:, qb, 0:97], p.t[0:NCMP, qb * 128:(qb + 1) * 128], RC.t[0:NCMP, g, 0:97], qb == 0, qb == 3, r=p.b + RC.b, w=acc.b)

                                def fin(acc=acc, rc_=rc_, cf_=cf_, i=i, h=h, c=c, octmp=octmp, imp=imp):
                                    av = acc.t[:, :].rearrange("p (q e) -> p q e", q=4)
                                    S.op("dve", lambda e: e.tensor_scalar_max(out=rc_.t[:, :], in0=av[:, :, 96], scalar1=1e-30), reads=acc.b, writes=rc_.b)
                                    S.op("dve", lambda e: e.reciprocal(out=rc_.t[:, :], in_=rc_.t[:, :]), reads=rc_.b, writes=rc_.b)
                                    S.op("dve", lambda e: e.tensor_tensor(out=cf_.t[:, :], in0=rc_.t[:, :], in1=gates.t[:, 4 * c:4 * c + 4, 3 * h], op=ALU.mult), reads=rc_.b + gates.b, writes=cf_.b)
                                    for qb in range(4):
                                        S.op("dve", lambda e: e.tensor_scalar_mul(out=octmp.t[:, qb, i, :], in0=av[:, qb, 0:64], scalar1=cf_.t[:, qb:qb + 1]), reads=acc.b + cf_.b, writes=octmp.b)
                                        if i == 0:
                                            S.op("dve", lambda e: e.tensor_scalar_mul(out=imp.t[:, qb, :], in0=av[:, qb, 64:96], scalar1=rc_.t[:, qb:qb + 1]), reads=acc.b + rc_.b, writes=imp.b)
                                        else:
                                            S.op("dve", lambda e: e.scalar_tensor_tensor(out=imp.t[:, qb, :], in0=av[:, qb, 64:96], scalar=rc_.t[:, qb:qb + 1], in1=imp.t[:, qb, :],
                                                                                       op0=ALU.mult, op1=ALU.add), reads=acc.b + rc_.b + imp.b, writes=imp.b)
                                units.append(dict(qk=qk, exp=ex, pv=pv, fin=fin))
                        self.run_units(cunits)
                        for g in range(2):
                            imp = imps[g]
                            for qb in range(0 if "notopk" in _DBG else 4):
                                t = 4 * c + qb
                                S.op("dve", lambda e: e.tensor_tensor(out=imp2.t[:, qb, :], in0=imp.t[:, qb, :], in1=keep.t[:, t * 32:(t + 1) * 32], op=ALU.mult), reads=imp.b + keep.b, writes=imp2.b)
                                S.op("dve", lambda e: e.tensor_tensor(out=imp2.t[:, qb, :], in0=imp2.t[:, qb, :], in1=addc.t[:, t * 32:(t + 1) * 32], op=ALU.add), reads=imp2.b + addc.b, writes=imp2.b)
                                S.op("dve", lambda e: e.max(out=mx8.t[:, qb, :], in_=imp2.t[:, qb, :]), reads=imp2.b, writes=mx8.b)
                                nc0 = 64 if g == 0 else 0
                                S.op("dve", lambda e: e.tensor_scalar(out=negb.t[:, qb, :], in0=imp2.t[:, qb, :], scalar1=mx8.t[:, qb, 7:8], scalar2=NEGM, op0=ALU.is_lt, op1=ALU.mult),
                                     reads=imp2.b + mx8.b, writes=negb.b)
                                self.tr(self.psb.t[0:32, qb * 128:(qb + 1) * 128], negb.t[:, qb, :], self.identb.t[:, :], r=negb.b + self.identb.b, w=self.psb.b)
                            if "notopk" not in _DBG:
                                S.op("act", lambda e: e.copy(out=negT[g].t[0:32, :], in_=self.psb.t[0:32, 0:512]), reads=self.psb.b, writes=negT[g].b)
                            for i in range(0 if "notopk" in _DBG else 4):
                                hq = g * 4 + i
                                S.dma(qn[hq].t[nc0:nc0 + 32, c * 512:(c + 1) * 512], negT[g].t[0:32, :], reads=negT[g].b, writes=[qn[hq].b[c]])
                        for g in range(2):
                            octmp = octmps[g]
                            nc0 = 64 if g == 0 else 0
                            units = []
                            for i in range(0 if "noselwin" in _DBG else 4):
                                h = g * 4 + i
                                base = g * 64
                                acc = self.ps[3 + fn % 4]
                                rc_, cf_ = rcp[fn % 3], coef[fn % 3]
                                os_ = osum[i % 2]
                                fn += 1
                                nk = 4 * c + 4
                                for kt in range(nk):
                                    u_ = self.ucount
                                    self.ucount += 1
                                    sc = self.ps[u_ % 3]
                                    p = pT[u_ % 3]
                                    col0 = 0 if kt < 4 * c else (kt - 4 * c) * 128
                                    diag = kt >= 4 * c

                                    def qk(sc=sc, kt=kt, c=c, col0=col0, h=h, g=g):
                                        self.mm(sc.t[:, col0:512], ksl[g].t[:, kt * 128:(kt + 1) * 128], qn[h].t[:, c * 512 + col0:(c + 1) * 512], True, True,
                                                r=[ksl[g].b[kt // 4], ksl[g].b[4], qn[h].b[c]], w=sc.b)

                                    def ex(sc=sc, p=p, col0=col0, diag=diag):
                                        self.act(p.t[:, col0:512], sc.t[:, col0:512], AF.Exp, r=sc.b, w=p.b, scale=0.125)
                                        if diag:
                                            self.causal_fix(p, col0)

                                    def pv(p=p, kt=kt, c=c, col0=col0, acc=acc, g=g):
                                        av = acc.t[:, :].rearrange("p (q e) -> p q e", q=4)
                                        for qb in range(col0 // 128, 4):
                                            self.mm(av[:, qb, 0:65], p.t[:, qb * 128:(qb + 1) * 128], vsl.t[:, kt, g, 0:65],
                                                    kt == 0 and qb == 0, kt == 4 * c + 3 and qb == 3, r=p.b + vsl.b, w=acc.b)
                                    unit = dict(qk=qk, exp=ex, pv=pv)
                                    if kt == nk - 1:
                                        def fin(acc=acc, rc_=rc_, cf_=cf_, i=i, h=h, c=c, os_=os_):
                                            av = acc.t[:, :].rearrange("p (q e) -> p q e", q=4)
                                            S.op("dve", lambda e: e.reciprocal(out=rc_.t[:, :], in_=av[:, :, 64]), reads=acc.b, writes=rc_.b)
                                            S.op("dve", lambda e: e.tensor_tensor(out=cf_.t[:, :], in0=rc_.t[:, :], in1=gates.t[:, 4 * c:4 * c + 4, 3 * h + 1], op=ALU.mult), reads=rc_.b + gates.b, writes=cf_.b)
                                            for qb in range(4):
                                                S.op("dve", lambda e: e.scalar_tensor_tensor(out=os_.t[:, qb, :], in0=av[:, qb, 0:64], scalar=cf_.t[:, qb:qb + 1], in1=octmp.t[:, qb, i, :],
                                                                                           op0=ALU.mult, op1=ALU.add), reads=acc.b + cf_.b + octmp.b, writes=os_.b)
                                        unit["fin"] = fin
                                    units.append(unit)
                                acc = self.ps[3 + fn % 4]
                                rc_, cf_ = rcp[fn % 3], coef[fn % 3]
                                fn += 1
                                kts = list(range(max(0, 4 * c - 4), 4 * c + 4))
                                first = True
                                for kt in kts:
                                    m = kt - (4 * c - 4)
                                    i_lo, i_hi = max(0, m - 4), min(3, m)
                                    u_ = self.ucount
                                    self.ucount += 1
                                    sc = self.ps[u_ % 3]
                                    p = pT[u_ % 3]
                                    c0, c1 = i_lo * 128, (i_hi + 1) * 128

                                    def qk(sc=sc, kt=kt, c=c, c0=c0, c1=c1, h=h, g=g):
                                        self.mm(sc.t[:, c0:c1], kw[g].t[:, kt * 128:(kt + 1) * 128], qn[h].t[:, c * 512 + c0:c * 512 + c1], True, True,
                                                r=[kw[g].b[kt // 4], qn[h].b[c]], w=sc.b)

                                    def ex(sc=sc, p=p, c0=c0, c1=c1, m=m, i_lo=i_lo, i_hi=i_hi):
                                        self.act(p.t[:, c0:c1], sc.t[:, c0:c1], AF.Exp, r=sc.b, w=p.b, scale=0.125)
                                        if m >= 4:
                                            self.causal_fix(p, i_lo * 128)
                                        if m <= 3:
                                            blk = p.t[:, i_hi * 128:(i_hi + 1) * 128]
                                            S.op("pool", lambda e: e.affine_select(out=blk, in_=blk, pattern=[[-1, 128]], compare_op=ALU.is_gt, fill=0.0,
                                                                                    base=0, channel_multiplier=1), reads=p.b, writes=p.b)

                                    def pv(p=p, kt=kt, c=c, i_lo=i_lo, i_hi=i_hi, acc=acc, g=g, first=first):
                                        av = acc.t[:, :].rearrange("p (q e) -> p q e", q=4)
                                        for qb in range(i_lo, i_hi + 1):
                                            self.mm(av[:, qb, 0:65], p.t[:, qb * 128:(qb + 1) * 128], vw.t[:, kt, g, 0:65],
                                                    first and qb == i_lo, kt == 4 * c + 3 and qb == 3, r=p.b + vw.b, w=acc.b)
                                    first = False
                                    unit = dict(qk=qk, exp=ex, pv=pv)
                                    if kt == kts[-1]:
                                        def fin(acc=acc, rc_=rc_, cf_=cf_, i=i, h=h, c=c, os_=os_):
                                            av = acc.t[:, :].rearrange("p (q e) -> p q e", q=4)
                                            S.op("dve", lambda e: e.reciprocal(out=rc_.t[:, :], in_=av[:, :, 64]), reads=acc.b, writes=rc_.b)
                                            S.op("dve", lambda e: e.tensor_tensor(out=cf_.t[:, :], in0=rc_.t[:, :], in1=gates.t[:, 4 * c:4 * c + 4, 3 * h + 2], op=ALU.mult), reads=rc_.b + gates.b, writes=cf_.b)
                                            for qb in range(4):
                                                S.op("dve", lambda e: e.scalar_tensor_tensor(out=otm.t[:, 4 * c + qb, 512 + h * 64:512 + (h + 1) * 64], in0=av[:, qb, 0:64], scalar=cf_.t[:, qb:qb + 1],
                                                                                           in1=os_.t[:, qb, :], op0=ALU.mult, op1=ALU.add), reads=acc.b + cf_.b + os_.b, writes=[otm.b[4 * c + qb]])
                                        unit["fin"] = fin
                                    units.append(unit)
                            self.run_units(units)
                    S.barrier()
            with ExitStack() as st1:
                wA = [self.alloc(st1, f"dwA{i}", [128, 8, 128], BF16) for i in range(4)]
                wB = [self.alloc(st1, f"dwB{i}", [128, 8, 128], BF16) for i in range(4)]
                wV = [self.alloc(st1, f"dwV{i}", [128, 8, 128], BF16) for i in range(2)]
                qa = [self.alloc(st1, f"qa{i}", [128, SEQ], BF16, nb=4) for i in range(2)]
                ka = [[self.alloc(st1, f"ka{i}{cc}", [128, SEQ], BF16, nb=4) for cc in range(2)] for i in range(2)]
                for i in range(2):
                    for cc in range(2):
                        S.op("pool", lambda e: e.memset(ka[i][cc].t[:, :], 0.0), writes=ka[i][cc].b)
                vaa = [self.alloc(st1, f"vaa{i}", [128, NT, 130], BF16) for i in range(2)]
                a0 = self.alloc(st1, "a0", [128, 4, 128], F32)
                ods = [self.alloc(st1, f"od{i}", [128, 4, 128], F32) for i in range(2)]
                junk = self.alloc(st1, "junk", [128, 128], F32)
                sss = [self.alloc(st1, f"ss{i}", [128, 4], F32) for i in range(2)]
                rcp = [self.alloc(st1, f"drcp{i}", [128, 4], F32) for i in range(2)]
                lp = self.alloc(st1, "lp", [128, 256], F32)
                lam = self.alloc(st1, "lam", [128, 4], F32)
                gsub = self.alloc(st1, "gsub", [128, 128], F32)
                for i in range(2):
                    S.op("pool", lambda e: e.memset(vaa[i].t[:, :, 128:130], 1.0), writes=vaa[i].b)
                S.dma(lp.t[:, :], self.d["diff_lambda"][li:li + 1].rearrange("o a d -> o (a d)").partition_broadcast(128), writes=lp.b)
                S.op("dve", lambda e: e.tensor_tensor(out=lp.t[:, 0:64], in0=lp.t[:, 0:64], in1=lp.t[:, 64:128], op=ALU.mult), reads=lp.b, writes=lp.b)
                S.op("dve", lambda e: e.tensor_tensor(out=lp.t[:, 128:192], in0=lp.t[:, 128:192], in1=lp.t[:, 192:256], op=ALU.mult), reads=lp.b, writes=lp.b)
                S.op("dve", lambda e: e.tensor_reduce(out=lam.t[:, 0:1], in_=lp.t[:, 0:64], axis=mybir.AxisListType.X, op=ALU.add), reads=lp.b, writes=lam.b)
                S.op("dve", lambda e: e.tensor_reduce(out=lam.t[:, 1:2], in_=lp.t[:, 128:192], axis=mybir.AxisListType.X, op=ALU.add), reads=lp.b, writes=lam.b)
                self.act(lam.t[:, 0:2], lam.t[:, 0:2], AF.Exp, r=lam.b, w=lam.b)
                S.op("dve", lambda e: e.tensor_tensor(out=lam.t[:, 2:3], in0=lam.t[:, 1:2], in1=lam.t[:, 0:1], op=ALU.subtract), reads=lam.b, writes=lam.b)
                S.op("dve", lambda e: e.tensor_scalar_add(out=lam.t[:, 2:3], in0=lam.t[:, 2:3], scalar1=-float(lam_init)), reads=lam.b, writes=lam.b)
                S.dma(gsub.t[:, :], self.d["diff_subln"][li:li + 1, :].partition_broadcast(128), writes=gsub.b)
                S.op("dve", lambda e: e.tensor_scalar_mul(out=gsub.t[:, :], in0=gsub.t[:, :], scalar1=float(1.0 - lam_init)), reads=gsub.b, writes=gsub.b)
                gn = 0
                def load_head(h):
                    j = h % 2
                    aq, bq, ak, bk, wv_ = wA[2 * j], wB[2 * j], wA[2 * j + 1], wB[2 * j + 1], wV[j]
                    self.load_w(aq.t[:, :, :], aq.b, Wr[:, :, h * 128:(h + 1) * 128], 8, 128)
                    self.load_w(bq.t[:, :, :], bq.b, Ws[:, :, h * 128:(h + 1) * 128], 8, 128)
                    self.load_w(ak.t[:, :, :], ak.b, Wr[:, :, (4 + h) * 128:(5 + h) * 128], 8, 128)
                    self.load_w(bk.t[:, :, :], bk.b, Ws[:, :, (4 + h) * 128:(5 + h) * 128], 8, 128)
                    self.load_w(wv_.t[:, :, :], wv_.b, Wn[:, :, _EVEN_OFF["va"] + h * 128:_EVEN_OFF["va"] + (h + 1) * 128], 8, 128)
                if "nodiff" not in _DBG:
                    load_head(0)
                for h in range(0 if "nodiff" in _DBG else 4):
                    j = h % 2
                    aq, bq, ak, bk, wv_ = wA[2 * j], wB[2 * j], wA[2 * j + 1], wB[2 * j + 1], wV[j]
                    self.proj_fm([aq, bq], None, self.rope_evac([(qa[j], 0, 128)], ropec, ropes, tmps))
                    self.proj_fm([ak, bk], None, self.rope_evac([(ka[j][0], 0, 64), (ka[j][1], 64, 128)], ropec, ropes, tmps))

                    def evv(t0, per, pb, j=j):
                        S.op("dve", lambda e: e.tensor_copy(out=vaa[j].t[:, t0:t0 + per, 0:128], in_=pb.t[:, :].rearrange("p (t d) -> p t d", t=per)), reads=pb.b, writes=vaa[j].b)
                    self.proj_tm(wv_, 128, evv)
                    if h + 1 < 4:
                        load_head(h + 1)
                    units = []
                    for c in range(4):
                        for cc in range(2):
                            base = cc * 64
                            accs = (self.ps[3], self.ps[4]) if gn % 2 == 0 else (self.ps[5], self.ps[6])
                            rc_ = rcp[gn % 2]
                            od, ss = ods[(gn // 2) % 2], sss[(gn // 2) % 2]
                            gn += 1
                            nk = 4 * c + 4
                            for kt in range(nk):
                                u_ = self.ucount
                                self.ucount += 1
                                sc = self.ps[u_ % 3]
                                p = pT[u_ % 3]
                                col0 = 0 if kt < 4 * c else (kt - 4 * c) * 128
                                diag = kt >= 4 * c

                                def qk(sc=sc, kt=kt, c=c, col0=col0, cc=cc, j=j):
                                    self.mm(sc.t[:, col0:512], ka[j][cc].t[:, kt * 128:(kt + 1) * 128], qa[j].t[:, c * 512 + col0:(c + 1) * 512], True, True,
                                            r=[ka[j][cc].b[kt // 4], qa[j].b[c]], w=sc.b)

                                def ex(sc=sc, p=p, col0=col0, diag=diag):
                                    self.act(p.t[:, col0:512], sc.t[:, col0:512], AF.Exp, r=sc.b, w=p.b, scale=0.125)
                                    if diag:
                                        self.causal_fix(p, col0)

                                def pv(p=p, kt=kt, c=c, col0=col0, accs=accs, j=j):
                                    for qb in range(col0 // 128, 4):
                                        bank = accs[qb // 2]
                                        av = bank.t[:, :].rearrange("p (q e) -> p q e", q=2)
                                        first = kt == 0 and qb % 2 == 0
                                        lastm = (kt == 4 * c + qb) and qb % 2 == 1
                                        self.mm(av[:, qb % 2, 0:129], p.t[:, qb * 128:(qb + 1) * 128], vaa[j].t[:, kt, 0:129], first, lastm, r=p.b + vaa[j].b, w=bank.b)
                                unit = dict(qk=qk, exp=ex, pv=pv)
                                if kt == nk - 1:
                                    def fin(accs=accs, rc_=rc_, cc=cc, c=c, h=h, od=od, ss=ss):
                                        for qb in range(4):
                                            bank = accs[qb // 2]
                                            av = bank.t[:, :].rearrange("p (q e) -> p q e", q=2)
                                            S.op("dve", lambda e: e.reciprocal(out=rc_.t[:, qb:qb + 1], in_=av[:, qb % 2, 128:129]), reads=bank.b, writes=rc_.b)
                                            if cc == 0:
                                                S.op("dve", lambda e: e.tensor_scalar_mul(out=a0.t[:, qb, :], in0=av[:, qb % 2, 0:128], scalar1=rc_.t[:, qb:qb + 1]), reads=bank.b + rc_.b, writes=a0.b)
                                            else:
                                                S.op("dve", lambda e: e.tensor_scalar_mul(out=od.t[:, qb, :], in0=av[:, qb % 2, 0:128], scalar1=rc_.t[:, qb:qb + 1]), reads=bank.b + rc_.b, writes=od.b)
                                        if cc == 1:
                                            for qb in range(4):
                                                S.op("dve", lambda e: e.scalar_tensor_tensor(out=od.t[:, qb, :], in0=od.t[:, qb, :], scalar=lam.t[:, 2:3], in1=a0.t[:, qb, :],
                                                                                           op0=ALU.mult, op1=ALU.add), reads=od.b + lam.b + a0.b, writes=od.b)
                                                S.op("dve", lambda e: e.tensor_tensor(out=junk.t[:, :], in0=od.t[:, qb, :], in1=od.t[:, qb, :], op=ALU.mult), reads=od.b, writes=junk.b)
                                                S.op("dve", lambda e: e.tensor_reduce(out=ss.t[:, qb:qb + 1], in_=junk.t[:, :], axis=mybir.AxisListType.X, op=ALU.add), reads=junk.b, writes=ss.b)

                                    def fin2(c=c, h=h, od=od, ss=ss):
                                        self.act(ss.t[:, :], ss.t[:, :], AF.Sqrt, r=ss.b + self.cst.b, w=ss.b, bias=self.cst.t[:, 1:2], scale=1.0 / 128.0)
                                        S.op("dve", lambda e: e.reciprocal(out=ss.t[:, :], in_=ss.t[:, :]), reads=ss.b, writes=ss.b)
                                        for qb in range(4):
                                            S.op("dve", lambda e: e.scalar_tensor_tensor(out=otm.t[:, 4 * c + qb, h * 128:(h + 1) * 128], in0=od.t[:, qb, :], scalar=ss.t[:, qb:qb + 1],
                                                                                       in1=gsub.t[:, :], op0=ALU.mult, op1=ALU.mult), reads=od.b + ss.b + gsub.b, writes=[otm.b[4 * c + qb]])
                                    unit["fin"] = fin
                                    if cc == 1:
                                        unit["fin2"] = fin2
                                units.append(unit)
                    self.run_units(units)
                pass
            self.otm_to_oT(otm)
            S.barrier()


_N_CORES = 8


def _run(inputs, n_seq, layers, n_cores, xs, trace=False):
    prog = Prog(n_seq, layers)
    nc = prog.build()
    extra = _host_layout(inputs)
    consts = _consts()
    base = {}
    for k in prog.d:
        if k == "x":
            continue
        if k in consts:
            base[k] = consts[k]
        elif k in extra:
            base[k] = np.ascontiguousarray(extra[k], dtype=np.float32)
        else:
            base[k] = np.ascontiguousarray(np.asarray(inputs[k]), dtype=np.float32)
    in_maps = []
    for c in range(n_cores):
        m = dict(base)
        m["x"] = np.ascontiguousarray(xs[c], dtype=np.float32)
        in_maps.append(m)
    if trace:
        res = run_bass_kernel_spmd(nc, in_maps, core_ids=list(range(n_cores)), trace=True)
        print("EXEC_NS", res.exec_time_ns)
    else:
        res = run_bass_kernel_spmd(nc, in_maps, core_ids=list(range(n_cores)))
    return [r["out"] for r in res.results]


def kernel(**inputs):
    x = np.asarray(inputs["x"], dtype=np.float32)
    B = x.shape[0]
    per = B // _N_CORES
    xs = [x[c * per:(c + 1) * per] for c in range(_N_CORES)]
    outs = _run(inputs, per, [0, 1, 2, 3], _N_CORES, xs)
    return np.concatenate(outs, axis=0).astype(np.float32)
```
